# Optimizing a Trainium2 kernel written in Bass

```python
import jax, jax.numpy as jnp
from jax import lax
import numpy as np

D_MODEL = 2048
BATCH = 16
SEQ = 2048
DEPTH = 4

CONV_WIDTH = D_MODEL // 4
CONV_GROUPS = 8
CONV_KERNEL = 31
POOL_WIDTH = D_MODEL // 4
POOL_WINDOWS = (2, 4, 8, 16)
POOL_GROUP = POOL_WIDTH // len(POOL_WINDOWS)
QK_NOPE_DIM = 128
QK_ROPE_DIM = 64
V_HEAD_DIM = 128
MLA_HEADS = (D_MODEL - CONV_WIDTH - POOL_WIDTH) // V_HEAD_DIM
MLA_WIDTH = MLA_HEADS * V_HEAD_DIM
Q_LORA_RANK = D_MODEL // 4
KV_LORA_RANK = D_MODEL // 8
MIX_WIDTH = MLA_WIDTH + CONV_WIDTH + POOL_WIDTH
IN_COLS = Q_LORA_RANK + KV_LORA_RANK + QK_ROPE_DIM + 2 * CONV_WIDTH + POOL_WIDTH
D_FF = ((8 * D_MODEL // 3 + 255) // 256) * 256
FFN_CONV_KERNEL = 3
ROPE_THETA = 10000.0
Q_BLOCK = 128
LN_EPS = 1e-5
RMS_EPS = 1e-6
DEEPNORM_ALPHA = (2.0 * DEPTH) ** 0.25
DEEPNORM_BETA = (8.0 * DEPTH) ** -0.25

kernel_name = "hymba_style_mla_conformer_pool_hybrid"


def layer_norm(x, g, b):
    xf = x.astype(jnp.float32)
    mu = jnp.mean(xf, axis=-1, keepdims=True)
    var = jnp.mean(jnp.square(xf - mu), axis=-1, keepdims=True)
    return ((xf - mu) * lax.rsqrt(var + LN_EPS) * g.astype(jnp.float32) + b.astype(jnp.float32)).astype(x.dtype)


def rms_norm(x, g):
    xf = x.astype(jnp.float32)
    ms = jnp.mean(jnp.square(xf), axis=-1, keepdims=True)
    return (xf * lax.rsqrt(ms + RMS_EPS) * g.astype(jnp.float32)).astype(x.dtype)


def causal_dwconv(x, w, b):
    k, c = w.shape
    y = lax.conv_general_dilated(
        x, w[:, None, :].astype(x.dtype), window_strides=(1,), padding=[(k - 1, 0)],
        dimension_numbers=("NWC", "WIO", "NWC"), feature_group_count=c)
    return y + b.astype(x.dtype)


def rope_cos_sin(positions):
    inv = 1.0 / (ROPE_THETA ** (jnp.arange(0, QK_ROPE_DIM, 2, dtype=jnp.float32) / QK_ROPE_DIM))
    ang = positions.astype(jnp.float32)[..., None] * inv
    return jnp.cos(ang), jnp.sin(ang)


def apply_rope(x, cos, sin):
    xf = x.astype(jnp.float32)
    x1, x2 = jnp.split(xf, 2, axis=-1)
    return jnp.concatenate([x1 * cos - x2 * sin, x1 * sin + x2 * cos], axis=-1).astype(x.dtype)


def mla_mixer(c_q, c_kv, k_rope, q_norm_g, w_uq, kv_norm_g, w_ukv, cos, sin):
    bsz, seq, _ = c_q.shape
    q = jnp.einsum("bsr,rhd->bshd", rms_norm(c_q, q_norm_g), w_uq)
    q_nope = q[..., :QK_NOPE_DIM]
    q_rope = apply_rope(q[..., QK_NOPE_DIM:], cos[:, :, None, :], sin[:, :, None, :])
    kv = jnp.einsum("bsr,rhd->bshd", rms_norm(c_kv, kv_norm_g), w_ukv)
    k_nope = kv[..., :QK_NOPE_DIM]
    v = kv[..., QK_NOPE_DIM:]
    k_rope = apply_rope(k_rope, cos, sin)
    qb = min(Q_BLOCK, seq)
    nb = seq // qb
    qn_blocks = q_nope.reshape(bsz, nb, qb, MLA_HEADS, QK_NOPE_DIM).transpose(1, 0, 2, 3, 4)
    qr_blocks = q_rope.reshape(bsz, nb, qb, MLA_HEADS, QK_ROPE_DIM).transpose(1, 0, 2, 3, 4)
    key_idx = jnp.arange(seq)
    scale = (QK_NOPE_DIM + QK_ROPE_DIM) ** -0.5
    neg = jnp.finfo(jnp.float32).min

    def attend(args):
        qn, qr, start = args
        s = (jnp.einsum("bqhd,bkhd->bhqk", qn, k_nope, preferred_element_type=jnp.float32)
             + jnp.einsum("bqhr,bkr->bhqk", qr, k_rope, preferred_element_type=jnp.float32)) * scale
        q_idx = start + jnp.arange(qb)
        s = jnp.where(key_idx[None, :] <= q_idx[:, None], s, neg)
        p = jax.nn.softmax(s, axis=-1)
        return jnp.einsum("bhqk,bkhd->bqhd", p.astype(v.dtype), v)

    out = lax.map(attend, (qn_blocks, qr_blocks, jnp.arange(nb) * qb))
    return out.transpose(1, 0, 2, 3, 4).reshape(bsz, seq, MLA_WIDTH)


def conformer_conv_mixer(u, conv_w, conv_b, ln_g, ln_b):
    a, g = jnp.split(u, 2, axis=-1)
    h = a * jax.nn.sigmoid(g)
    h = causal_dwconv(h, conv_w, conv_b)
    h = layer_norm(h, ln_g, ln_b)
    return jax.nn.silu(h)


def pool_mixer(u, w_pool, scale):
    bsz, seq, c = u.shape
    uf = u.astype(jnp.float32)
    cs = jnp.concatenate([jnp.zeros((bsz, 1, c), jnp.float32), lax.cumsum(uf, axis=1)], axis=1)
    t = jnp.arange(seq)
    outs = []
    for gi, w in enumerate(POOL_WINDOWS):
        sl = slice(gi * POOL_GROUP, (gi + 1) * POOL_GROUP)
        lo = jnp.maximum(t + 1 - w, 0)
        win_sum = cs[:, 1:, sl] - cs[:, lo, sl]
        cnt = (t + 1 - lo).astype(jnp.float32)[None, :, None]
        outs.append(win_sum / cnt - uf[:, :, sl])
    d = jnp.stack(outs, axis=2).astype(u.dtype)
    y = jnp.einsum("bsgc,gcd->bsgd", d, w_pool).reshape(bsz, seq, c)
    return y * scale


def setup_inputs(seed: int = 0) -> dict:
    key = jax.random.key(seed)
    ks = jax.random.split(key, 24)
    f32 = jnp.float32

    def nrm(k, shape, s):
        return jax.random.normal(k, shape, f32) * s

    def gain(k, shape):
        return 1.0 + 0.02 * jax.random.normal(k, shape, f32)

    L = DEPTH
    x = jax.random.normal(ks[0], (BATCH, SEQ, D_MODEL), f32)
    positions = jnp.broadcast_to(jnp.arange(SEQ, dtype=jnp.int32)[None, :], (BATCH, SEQ))
    return {
        "x": x,
        "positions": positions,
        "ln_in_g": gain(ks[1], (D_MODEL,)),
        "ln_in_b": nrm(ks[2], (D_MODEL,), 0.02),
        "w_in": nrm(ks[3], (L, D_MODEL, IN_COLS), D_MODEL ** -0.5),
        "q_norm_g": gain(ks[4], (L, Q_LORA_RANK)),
        "w_uq": nrm(ks[5], (L, Q_LORA_RANK, MLA_HEADS, QK_NOPE_DIM + QK_ROPE_DIM), Q_LORA_RANK ** -0.5),
        "kv_norm_g": gain(ks[6], (L, KV_LORA_RANK)),
        "w_ukv": nrm(ks[7], (L, KV_LORA_RANK, MLA_HEADS, QK_NOPE_DIM + V_HEAD_DIM), KV_LORA_RANK ** -0.5),
        "conv_w": nrm(ks[8], (L, CONV_KERNEL, CONV_WIDTH), CONV_KERNEL ** -0.5),
        "conv_b": nrm(ks[9], (L, CONV_WIDTH), 0.02),
        "conv_ln_g": gain(ks[10], (L, CONV_WIDTH)),
        "conv_ln_b": nrm(ks[11], (L, CONV_WIDTH), 0.02),
        "w_pool": nrm(ks[12], (L, len(POOL_WINDOWS), POOL_GROUP, POOL_GROUP), POOL_GROUP ** -0.5),
        "pool_scale": gain(ks[13], (L, POOL_WIDTH)),
        "w_out": nrm(ks[14], (L, MIX_WIDTH, D_MODEL), DEEPNORM_BETA * MIX_WIDTH ** -0.5),
        "ln1_g": gain(ks[15], (L, D_MODEL)),
        "ln1_b": nrm(ks[16], (L, D_MODEL), 0.02),
        "w_up": nrm(ks[17], (L, D_MODEL, 2 * D_FF), D_MODEL ** -0.5),
        "ffn_conv_w": nrm(ks[18], (L, FFN_CONV_KERNEL, 2 * D_FF), FFN_CONV_KERNEL ** -0.5),
        "ffn_conv_b": nrm(ks[19], (L, 2 * D_FF), 0.02),
        "w_down": nrm(ks[20], (L, D_FF, D_MODEL), DEEPNORM_BETA * D_FF ** -0.5),
        "ln2_g": gain(ks[21], (L, D_MODEL)),
        "ln2_b": nrm(ks[22], (L, D_MODEL), 0.02),
    }


def reference(x, positions, ln_in_g, ln_in_b, w_in, q_norm_g, w_uq, kv_norm_g, w_ukv,
              conv_w, conv_b, conv_ln_g, conv_ln_b, w_pool, pool_scale, w_out, ln1_g, ln1_b,
              w_up, ffn_conv_w, ffn_conv_b, w_down, ln2_g, ln2_b):
    cos, sin = rope_cos_sin(positions)
    x = layer_norm(x, ln_in_g, ln_in_b)
    o1 = Q_LORA_RANK
    o2 = o1 + KV_LORA_RANK
    o3 = o2 + QK_ROPE_DIM
    o4 = o3 + 2 * CONV_WIDTH
    for l in range(DEPTH):
        h = jnp.einsum("bsd,dc->bsc", x, w_in[l])
        c_q, c_kv, k_rope = h[..., :o1], h[..., o1:o2], h[..., o2:o3]
        u_conv, u_pool = h[..., o3:o4], h[..., o4:]
        y_mla = mla_mixer(c_q, c_kv, k_rope, q_norm_g[l], w_uq[l], kv_norm_g[l], w_ukv[l], cos, sin)
        y_conv = conformer_conv_mixer(u_conv, conv_w[l], conv_b[l], conv_ln_g[l], conv_ln_b[l])
        y_pool = pool_mixer(u_pool, w_pool[l], pool_scale[l])
        mixed = jnp.concatenate([y_mla, y_conv, y_pool], axis=-1)
        y = jnp.einsum("bsc,cd->bsd", mixed, w_out[l])
        x = layer_norm(DEEPNORM_ALPHA * x + y, ln1_g[l], ln1_b[l])
        up = jnp.einsum("bsd,df->bsf", x, w_up[l])
        up = causal_dwconv(up, ffn_conv_w[l], ffn_conv_b[l])
        a, g = jnp.split(up, 2, axis=-1)
        y = jnp.einsum("bsf,fd->bsd", a * jax.nn.silu(g), w_down[l])
        x = layer_norm(DEEPNORM_ALPHA * x + y, ln2_g[l], ln2_b[l])
    return x
```

```python
import math
import contextlib
import numpy as np
import concourse.bass as bass
import concourse.mybir as mybir
from concourse.bass_utils import run_bass_kernel_spmd

F32 = mybir.dt.float32
BF16 = mybir.dt.bfloat16
I32 = mybir.dt.int32
AF = mybir.ActivationFunctionType
ALU = mybir.AluOpType

ENGS = ["pe", "act", "dve", "pool", "sp"]
D = 2048
T = 512
NH = 8
DFF = 5632
ALPHA = (2.0 * 4) ** 0.25
SCALE = 192 ** -0.5
IN_AUG = 2432
LN_EPS = 1e-5
RMS_EPS = 1e-6


class Prog:
    def __init__(self, n_dma_sems=16):
        self.streams = {e: [] for e in ENGS}
        self.cnt = {}
        self.seen = {e: {} for e in ENGS}
        self.last_w = {}
        self.readers = {}
        self.ndma = n_dma_sems
        self.dma_rr = 0

    def _need(self, eng, deps, key, val):
        if key == "pe" and eng == "pe":
            return
        if self.seen[eng].get(key, 0) >= val:
            return
        if deps.get(key, 0) < val:
            deps[key] = val

    def _need_all(self, eng, deps, lw):
        for (k, v) in lw:
            self._need(eng, deps, k, v)

    def op(self, eng, fn, reads=(), writes=(), dma=False, sw=None):
        psr = [r for r in reads if isinstance(r, tuple) and r[0] == "ps"]
        if psr:
            reads = [r for r in reads if not (isinstance(r, tuple) and r[0] == "ps")]
            writes = list(writes) + [r for r in psr if r not in writes]
        deps = {}
        for r in reads:
            lw = self.last_w.get(r)
            if lw:
                self._need_all(eng, deps, lw)
        for w in writes:
            lw = self.last_w.get(w)
            if lw:
                self._need_all(eng, deps, lw)
            rd = self.readers.get(w)
            if rd:
                for k, v in rd.items():
                    self._need(eng, deps, k, v)
        if sw is not None:
            key = ("sw", sw)
            assert key not in self.cnt
            val = 16
            inc = 16
            self.nsw = max(getattr(self, "nsw", 0), sw + 1)
        elif dma:
            key = ("dma", self.dma_rr)
            self.dma_rr = (self.dma_rr + 1) % self.ndma
            prev = self.cnt.get(key, 0)
            if prev:
                self._need(eng, deps, key, prev)
            val = prev + 16
            inc = 16
        else:
            key = eng
            val = self.cnt.get(eng, 0) + 1
            inc = 1
        self.cnt[key] = val
        st = self.streams[eng]
        for k2, v2 in deps.items():
            st.append(("w", k2, v2))
            self.seen[eng][k2] = v2
        st.append(("o", fn, key, inc))
        for r in reads:
            self.readers.setdefault(r, {})[key] = val
        for w in writes:
            self.last_w[w] = [(key, val)]
            self.readers[w] = {}

    def finish(self, eng="sp"):
        st = self.streams[eng]
        for k, v in self.cnt.items():
            if isinstance(k, tuple) and k[0] == "dma" and self.seen[eng].get(k, 0) < v:
                st.append(("w", k, v))
                self.seen[eng][k] = v

    def emit(self, nc):
        keys = list(ENGS) + [("dma", i) for i in range(self.ndma)] + [("sw", i) for i in range(getattr(self, "nsw", 0))]
        with contextlib.ExitStack() as es:
            sems = {}
            for k in keys:
                nm = k if isinstance(k, str) else "%s%d" % (k[0], k[1])
                sems[k] = es.enter_context(nc.semaphore("s_" + nm))
            block = es.enter_context(nc.Block())

            def run(engobj, stream):
                for it in stream:
                    if it[0] == "w":
                        engobj.wait_ge(sems[it[1]], it[2])
                    elif it[0] == "c":
                        engobj.sem_clear(sems[it[1]]).then_inc(sems["pool"], 1)
                    else:
                        it[1](engobj).then_inc(sems[it[2]], it[3])

            @block.tensor
            def _(e):
                run(e, self.streams["pe"])

            @block.scalar
            def _(e):
                run(e, self.streams["act"])

            @block.vector
            def _(e):
                run(e, self.streams["dve"])

            @block.gpsimd
            def _(e):
                run(e, self.streams["pool"])

            @block.sync
            def _(e):
                run(e, self.streams["sp"])


def vec_layout(L):
    off = {}
    n = 0

    def add(name, w):
        nonlocal n
        off[name] = n
        n += w

    add("ln_in_g", 16)
    add("ln_in_b", 16)
    add("invfreq", 1)
    add("sinsign", 1)
    add("invcnt", 16)
    add("eps_ln", 1)
    add("eps_rms", 1)
    for l in range(L):
        for nm, w in [("ln1_g", 16), ("ln1_b", 16), ("ln2_g", 16), ("ln2_b", 16), ("qg", 4), ("kvg", 2),
                      ("conv_w", 124), ("conv_b", 4), ("cln_g", 4), ("cln_b", 4), ("pscale", 4),
                      ("fw", 264), ("fb", 88)]:
            add((nm, l), w)
    return off, n


def build(L, S, NSEQ):
    NT = S // T
    voff, NV = vec_layout(L)
    nc = bass.Bass("TRN2", target_bir_lowering=False)

    def din(name, shape, dt=F32):
        return nc.dram_tensor(name, shape, dt, kind="ExternalInput").ap()

    xT = din("xT", [NSEQ * D, S])
    pos = din("pos", [NSEQ, S], I32)
    w_in = din("w_in", [L * D, IN_AUG])
    w_uq = din("w_uq", [L * 512, 2048])
    w_ukv = din("w_ukv", [L * 256, 2048])
    w_out = din("w_out", [L * D, D])
    w_up = din("w_up", [L * D, 2 * DFF])
    w_down = din("w_down", [L * DFF, D])
    w_pool = din("w_pool", [L * 512, 128])
    vecs_d = din("vecs", [128, NV])
    tri_d = din("tri", [128, 128])
    outT = nc.dram_tensor("outT", [NSEQ * D, S], F32, kind="ExternalOutput").ap()

    P = Prog()
    es = contextlib.ExitStack()
    wsrc = {"w_in": w_in, "w_uq": w_uq, "w_ukv": w_ukv, "w_out": w_out, "w_up": w_up, "w_down": w_down, "w_pool": w_pool}
    wrows = {"w_in": D, "w_uq": 512, "w_ukv": 256, "w_out": D, "w_up": D, "w_down": DFF, "w_pool": 512}
    wbf = {}
    for nm, ap_ in wsrc.items():
        wbf[nm] = nc.dram_tensor(nm + "_bf", list(ap_.shape), BF16).ap()
    ncv = 0
    for l in range(L):
        for nm in ["w_pool", "w_in", "w_uq", "w_ukv", "w_out", "w_up", "w_down"]:
            r0, r1 = l * wrows[nm], (l + 1) * wrows[nm]
            P.op("pool", lambda e, nm=nm, r0=r0, r1=r1: e.dma_start(out=wbf[nm][r0:r1, :], in_=wsrc[nm][r0:r1, :], max_dma_last_dim=8192),
                 writes=[("wbf", nm, l)], sw=ncv)
            ncv += 1

    def sb(name, shape, dt):
        return es.enter_context(nc.sbuf_tensor(name, shape, dt))

    xres = sb("xres", [128, 16, T], F32)
    U = sb("U", [128, 40, T], BF16)
    cache_ckv = [sb("cckv%d" % l, [128, 2, 3 * T], BF16) for l in range(L)]
    cache_kr = [sb("ckr%d" % l, [64, 3 * T], BF16) for l in range(L)]
    kr_cur = sb("kr_cur", [64, T], BF16)
    slots = [sb("ws0", [128, 8192], BF16), sb("ws1", [128, 8192], BF16), sb("ws2", [128, 4096], BF16)]
    wpool_sb = sb("wpool", [128, 4, 128], BF16)
    fa = sb("fa", [128, 8, 544], F32)
    sig = sb("sig", [128, T], F32)
    rt = sb("rt", [128, 2, T], F32)
    mean_sb = sb("mean_sb", [128, T], F32)
    rstd_sb = sb("rstd_sb", [128, T], F32)
    lt = sb("lt", [128, 2, T], F32)
    rcp = lt[:, 0, :]
    cos_t = sb("cos_t", [64, T], F32)
    sin_t = sb("sin_t", [64, T], F32)
    posi = sb("posi", [64, T], I32)
    zb = sb("zb", [128, 4, T], BF16)
    vecs = sb("vecs_sb", [128, NV], F32)
    tri_f = sb("tri_f", [128, 128], F32)
    tri = sb("tri_b", [128, 128], BF16)
    ones1 = sb("ones1", [128, 128], BF16)
    onesN = sb("onesN", [128, 128], BF16)
    halo_glu = sb("halo_glu", [128, L * 4, 30], F32)
    halo_pool = sb("halo_pool", [128, L * 4, 15], F32)
    halo_up = sb("halo_up", [128, L * 88, 2], F32)
    ps = [es.enter_context(nc.psum_tensor("ps%d" % i, [128, T], F32)) for i in range(8)]

    def vcol(name, c=0, n=1):
        o = voff[name] + c
        return vecs[:, o:o + n]

    rot_state = {"set": [0, 1, 2, 3, 4], "i": 0}

    def rot():
        s = rot_state["set"]
        b = s[rot_state["i"] % len(s)]
        rot_state["i"] += 1
        return b

    def set_rot(s):
        rot_state["set"] = s
        rot_state["i"] = 0

    def mm_group(bank, out_ap, pairs, reads):
        def fn(e, pairs=pairs, out_ap=out_ap):
            n = len(pairs)
            ins = None
            for i, (l_, r_) in enumerate(pairs):
                ins = e.matmul(out_ap, l_, r_, start=(i == 0), stop=(i == n - 1))
            return ins
        P.op("pe", fn, reads=reads, writes=[("ps", bank)])

    UR = lambda i: ("u", i)

    def gen_blocks():
        for b in range(NSEQ):
            for j in range(NT):
                for l in range(L):
                    yield ("wp", "w_pool", l * 512, 4, 0, 128)
                    for (c0, C) in [(0, 512), (512, 384), (896, 512), (1408, 512), (1920, 512)]:
                        yield ("big", "w_in", l * D, 16, c0, C)
                    yield ("big", "w_uq", l * 512, 4, 0, 2048)
                    yield ("ukv", "w_ukv", l * 256, 2, 0, 2048)
                    for i in range(4):
                        yield ("big", "w_out", l * D, 16, i * 512, 512)
                    for half in range(2):
                        for i in range(11):
                            yield ("big", "w_up", l * D, 16, (half * 11 + i) * 512, 512)
                        for i in range(8):
                            yield ("big", "w_down", l * DFF + half * 2816, 22, i * 256, 256)

    blocks = list(gen_blocks())
    wstate = {"next": 0, "issued": 0, "big": 0}
    wviews = {}

    def w_issue(i):
        kind, wname, r0, KC, c0, C = blocks[i]
        dram2d = wbf[wname]
        lyr = r0 // wrows[wname]
        if kind == "big":
            s = wstate["big"] % 2
            wstate["big"] += 1
            res = ("ws", s)
            view = slots[s][:, 0:KC * C].rearrange("p (kc c) -> p kc c", kc=KC)
        elif kind == "ukv":
            res = ("ws", 2)
            view = slots[2][:, 0:KC * C].rearrange("p (kc c) -> p kc c", kc=KC)
        else:
            res = "wpool"
            view = wpool_sb[:, :, :]
        src = dram2d[r0:r0 + KC * 128, c0:c0 + C].rearrange("(kc p) c -> p kc c", p=128)
        P.op("sp", lambda e, view=view, src=src: e.dma_start(out=view, in_=src), reads=[("wbf", wname, lyr)], writes=[res], dma=True)
        wviews[i] = (view, res)

    def wnext(kind):
        i = wstate["next"]
        assert blocks[i][0] == kind, (blocks[i][0], kind)
        wstate["next"] += 1
        lim = i
        nb = 0
        k = i + 1
        while k < len(blocks) and nb < 1:
            if blocks[k][0] == "big":
                nb += 1
            lim = k
            k += 1
        while wstate["issued"] <= lim:
            w_issue(wstate["issued"])
            wstate["issued"] += 1
        return wviews.pop(i)

    def ckv_ap(l, blk, kc):
        if blk < 3:
            return cache_ckv[l][:, kc, blk * T:(blk + 1) * T], ("ckv", l, blk, kc)
        return U[:, 38 + kc, :], UR(38 + kc)

    def kr_ap(l, blk):
        if blk < 3:
            return cache_kr[l][:, blk * T:(blk + 1) * T], ("kr", l, blk)
        return kr_cur[:, :], "kr_cur"

    P.op("sp", lambda e: e.dma_start(out=vecs[:, :], in_=vecs_d), writes=["vecs"], dma=True)
    P.op("sp", lambda e: e.dma_start(out=tri_f[:, :], in_=tri_d), writes=["tri_f"], dma=True)
    P.op("act", lambda e: e.activation(out=tri[:, :], in_=tri_f[:, :], func=AF.Copy), reads=["tri_f"], writes=["tri"])
    P.op("dve", lambda e: e.memset(ones1[:, :], 1.0), writes=["ones1"])
    P.op("dve", lambda e: e.memset(onesN[:, :], 1.0 / D), writes=["onesN"])

    def layer_norm(gname, bname):
        xr_all = [("xr", c) for c in range(16)]
        for c in range(16):
            i = c % 2
            P.op("act", lambda e, c=c, i=i: e.activation(out=zb[:, i, :], in_=xres[:, c, :], func=AF.Copy),
                 reads=[("xr", c)], writes=[("zb", i)])
            P.op("act", lambda e, c=c, i=i: e.activation(out=zb[:, 2 + i, :], in_=xres[:, c, :], func=AF.Square),
                 reads=[("xr", c)], writes=[("zb", 2 + i)])

            def fn(e, c=c, i=i):
                e.matmul(ps[5][:, :], onesN[:, :], zb[:, i, :], start=(c == 0), stop=(c == 15))
                return e.matmul(ps[6][:, :], onesN[:, :], zb[:, 2 + i, :], start=(c == 0), stop=(c == 15))
            P.op("pe", fn, reads=[("zb", i), ("zb", 2 + i), "onesN"], writes=[("ps", 5), ("ps", 6)])
        stats_finish(5, 6)
        for c in range(16):
            i = c % 2
            P.op("dve", lambda e, c=c, i=i: e.tensor_tensor(out=lt[:, i, :], in0=xres[:, c, :], in1=mean_sb[:, :], op=ALU.subtract),
                 reads=[("xr", c), "mean"], writes=[("lt", i)])
            P.op("dve", lambda e, i=i: e.tensor_tensor(out=lt[:, i, :], in0=lt[:, i, :], in1=rstd_sb[:, :], op=ALU.mult),
                 reads=[("lt", i), "rstd"], writes=[("lt", i)])
            P.op("act", lambda e, c=c, i=i: e.activation(out=xres[:, c, :], in_=lt[:, i, :], func=AF.Identity,
                                                         scale=vcol(gname, c), bias=vcol(bname, c)),
                 reads=[("lt", i), "vecs"], writes=[("xr", c)])
            P.op("act", lambda e, c=c, i=i: e.activation(out=U[:, c, :], in_=lt[:, i, :], func=AF.Identity,
                                                         scale=vcol(gname, c), bias=vcol(bname, c)),
                 reads=[("lt", i), "vecs"], writes=[UR(c)])

    def stats_finish(bm, bq):
        P.op("dve", lambda e: e.tensor_copy(out=mean_sb[:, :], in_=ps[bm][:, :]), reads=[("ps", bm)], writes=["mean"])
        P.op("dve", lambda e: e.tensor_tensor(out=rcp, in0=mean_sb[:, :], in1=mean_sb[:, :], op=ALU.mult),
             reads=["mean"], writes=[("lt", 0)])
        P.op("dve", lambda e: e.tensor_tensor(out=rstd_sb[:, :], in0=ps[bq][:, :], in1=rcp, op=ALU.subtract),
             reads=[("ps", bq), ("lt", 0)], writes=["rstd"])
        P.op("act", lambda e: e.activation(out=rstd_sb[:, :], in_=rstd_sb[:, :], func=AF.Sqrt, bias=vcol("eps_ln"), scale=1.0),
             reads=["rstd", "vecs"], writes=["rstd"])
        P.op("dve", lambda e: e.reciprocal(out=rstd_sb[:, :], in_=rstd_sb[:, :]), reads=["rstd"], writes=["rstd"])

    def range_reduce(buf, res):
        t1 = rt[0:64, 1, :]
        P.op("dve", lambda e: e.tensor_scalar(out=t1, in0=buf, scalar1=1.0 / (2 * math.pi), scalar2=None, op0=ALU.mult),
             reads=[res], writes=[("rt", 1)])
        P.op("dve", lambda e: e.tensor_copy(out=posi[:, :], in_=t1), reads=[("rt", 1)], writes=["posi"])
        P.op("dve", lambda e: e.tensor_copy(out=t1, in_=posi[:, :]), reads=["posi"], writes=[("rt", 1)])
        P.op("dve", lambda e: e.scalar_tensor_tensor(out=buf, in0=t1, scalar=-2 * math.pi, in1=buf, op0=ALU.mult, op1=ALU.add),
             reads=[("rt", 1), res], writes=[res])
        P.op("dve", lambda e: e.tensor_scalar(out=t1, in0=buf, scalar1=math.pi, scalar2=2 * math.pi, op0=ALU.is_gt, op1=ALU.mult),
             reads=[res], writes=[("rt", 1)])
        P.op("dve", lambda e: e.tensor_tensor(out=buf, in0=buf, in1=t1, op=ALU.subtract), reads=[res, ("rt", 1)], writes=[res])
        P.op("dve", lambda e: e.tensor_scalar(out=t1, in0=buf, scalar1=-math.pi, scalar2=2 * math.pi, op0=ALU.is_lt, op1=ALU.mult),
             reads=[res], writes=[("rt", 1)])
        P.op("dve", lambda e: e.tensor_tensor(out=buf, in0=buf, in1=t1, op=ALU.add), reads=[res, ("rt", 1)], writes=[res])

    def rope(ps_a, ps_b, dst, dst_res):
        P.op("dve", lambda e: e.tensor_tensor(out=rt[0:64, 0, :], in0=ps[ps_a][0:64, :], in1=cos_t[:, :], op=ALU.mult),
             reads=[("ps", ps_a), "cos"], writes=[("rt", 0)])
        P.op("dve", lambda e: e.tensor_tensor(out=rt[0:64, 1, :], in0=ps[ps_b][0:64, :], in1=sin_t[:, :], op=ALU.mult),
             reads=[("ps", ps_b), "sin"], writes=[("rt", 1)])
        P.op("dve", lambda e: e.tensor_tensor(out=dst, in0=rt[0:64, 0, :], in1=rt[0:64, 1, :], op=ALU.add),
             reads=[("rt", 0), ("rt", 1)], writes=[dst_res])

    XB = [UR(c) for c in range(16)]

    for b in range(NSEQ):
        for j in range(NT):
            t0 = j * T
            P.op("sp", lambda e, b=b, t0=t0: e.dma_start(
                out=xres[:, :, :], in_=xT[b * D:(b + 1) * D, t0:t0 + T].rearrange("(c p) t -> p c t", p=128)),
                writes=[("xr", c) for c in range(16)], dma=True)
            P.op("sp", lambda e, b=b, t0=t0: e.dma_start(out=posi[:, :], in_=pos[b:b + 1, t0:t0 + T].partition_broadcast(64)),
                 writes=["posi"], dma=True)
            if j == 0:
                P.op("dve", lambda e: e.memset(halo_glu[:, :, :], 0.0), writes=["hglu"])
                P.op("dve", lambda e: e.memset(halo_pool[:, :, :], 0.0), writes=["hpool"])
                P.op("dve", lambda e: e.memset(halo_up[:, :, :], 0.0), writes=["hup"])
            P.op("dve", lambda e: e.tensor_copy(out=sin_t[:, :], in_=posi[:, :]), reads=["posi"], writes=["sin"])
            P.op("dve", lambda e: e.tensor_scalar(out=sin_t[:, :], in0=sin_t[:, :], scalar1=vecs[0:64, voff["invfreq"]:voff["invfreq"] + 1],
                                                  scalar2=None, op0=ALU.mult), reads=["sin", "vecs"], writes=["sin"])
            P.op("dve", lambda e: e.tensor_scalar(out=cos_t[:, :], in0=sin_t[:, :], scalar1=math.pi / 2, scalar2=None, op0=ALU.add),
                 reads=["sin"], writes=["cos"])
            range_reduce(cos_t[:, :], "cos")
            P.op("act", lambda e: e.activation(out=cos_t[:, :], in_=cos_t[:, :], func=AF.Sin), reads=["cos"], writes=["cos"])
            range_reduce(sin_t[:, :], "sin")
            P.op("act", lambda e: e.activation(out=sin_t[:, :], in_=sin_t[:, :], func=AF.Sin), reads=["sin"], writes=["sin"])
            P.op("dve", lambda e: e.tensor_scalar(out=sin_t[:, :], in0=sin_t[:, :], scalar1=vecs[0:64, voff["sinsign"]:voff["sinsign"] + 1],
                                                  scalar2=None, op0=ALU.mult), reads=["sin", "vecs"], writes=["sin"])
            set_rot([0, 1, 2, 3, 4])
            layer_norm("ln_in_g", "ln_in_b")

            for l in range(L):
                V_ = lambda nm, c=0, l=l: vcol((nm, l), c)
                set_rot([0, 1, 2, 3, 4])
                wp_view, wp_res = wnext("wp")
                wv, wr = wnext("big")
                for c in range(4):
                    bk = rot()
                    mm_group(bk, ps[bk][:, :], [(wv[:, kc, c * 128:(c + 1) * 128], U[:, kc, :]) for kc in range(16)], [wr] + XB)
                    P.op("act", lambda e, V_=V_, c=c, bk=bk: e.activation(out=fa[:, c, 0:T], in_=ps[bk][:, :], func=AF.Copy),
                         reads=[("ps", bk)], writes=[("fa", c)])
                    P.op("act", lambda e, V_=V_, c=c, bk=bk: e.activation(out=U[:, 35 + c % 2, :], in_=ps[bk][:, :], func=AF.Square),
                         reads=[("ps", bk)], writes=[UR(35 + c % 2)])
                    P.op("pe", lambda e, V_=V_, c=c: e.matmul(ps[7][:, :], ones1[:, :], U[:, 35 + c % 2, :], start=(c == 0), stop=(c == 3)),
                         reads=[UR(35 + c % 2), "ones1"], writes=[("ps", 7)])
                P.op("act", lambda e, V_=V_: e.activation(out=rstd_sb[:, :], in_=ps[7][:, :], func=AF.Sqrt, bias=vcol("eps_rms"), scale=1.0 / 512),
                     reads=[("ps", 7), "vecs"], writes=["rstd"])
                P.op("dve", lambda e, V_=V_: e.reciprocal(out=rstd_sb[:, :], in_=rstd_sb[:, :]), reads=["rstd"], writes=["rstd"])
                for c in range(4):
                    P.op("dve", lambda e, V_=V_, c=c: e.scalar_tensor_tensor(out=U[:, 16 + c, :], in0=fa[:, c, 0:T], scalar=V_("qg", c),
                                                                      in1=rstd_sb[:, :], op0=ALU.mult, op1=ALU.mult),
                         reads=[("fa", c), "rstd", "vecs"], writes=[UR(16 + c)])
                wv, wr = wnext("big")
                for c in range(2):
                    bk = rot()
                    mm_group(bk, ps[bk][:, :], [(wv[:, kc, c * 128:(c + 1) * 128], U[:, kc, :]) for kc in range(16)], [wr] + XB)
                    P.op("act", lambda e, V_=V_, c=c, bk=bk: e.activation(out=fa[:, 4 + c, 0:T], in_=ps[bk][:, :], func=AF.Copy),
                         reads=[("ps", bk)], writes=[("fa", 4 + c)])
                    P.op("act", lambda e, V_=V_, c=c, bk=bk: e.activation(out=U[:, 35 + c % 2, :], in_=ps[bk][:, :], func=AF.Square),
                         reads=[("ps", bk)], writes=[UR(35 + c % 2)])
                    P.op("pe", lambda e, V_=V_, c=c: e.matmul(ps[7][:, :], ones1[:, :], U[:, 35 + c % 2, :], start=(c == 0), stop=(c == 1)),
                         reads=[UR(35 + c % 2), "ones1"], writes=[("ps", 7)])
                P.op("act", lambda e, V_=V_: e.activation(out=mean_sb[:, :], in_=ps[7][:, :], func=AF.Sqrt, bias=vcol("eps_rms"), scale=1.0 / 256),
                     reads=[("ps", 7), "vecs"], writes=["mean"])
                P.op("dve", lambda e, V_=V_: e.reciprocal(out=mean_sb[:, :], in_=mean_sb[:, :]), reads=["mean"], writes=["mean"])
                for c in range(2):
                    dst, dres = ckv_ap(l, j, c)
                    P.op("dve", lambda e, V_=V_, c=c, dst=dst: e.scalar_tensor_tensor(out=dst, in0=fa[:, 4 + c, 0:T], scalar=V_("kvg", c),
                                                                               in1=mean_sb[:, :], op0=ALU.mult, op1=ALU.mult),
                         reads=[("fa", 4 + c), "mean", "vecs"], writes=[dres])
                bka, bkb = rot(), rot()
                mm_group(bka, ps[bka][0:64, :], [(wv[:, kc, 256:320], U[:, kc, :]) for kc in range(16)], [wr] + XB)
                mm_group(bkb, ps[bkb][0:64, :], [(wv[:, kc, 320:384], U[:, kc, :]) for kc in range(16)], [wr] + XB)
                dst, dres = kr_ap(l, j)
                rope(bka, bkb, dst, dres)
                for blk2 in range(2):
                    wv, wr = wnext("big")
                    for ci in range(2):
                        c = blk2 * 2 + ci
                        bka, bkb = rot(), rot()
                        mm_group(bka, ps[bka][:, :], [(wv[:, kc, ci * 256:ci * 256 + 128], U[:, kc, :]) for kc in range(16)], [wr] + XB)
                        mm_group(bkb, ps[bkb][:, :], [(wv[:, kc, ci * 256 + 128:ci * 256 + 256], U[:, kc, :]) for kc in range(16)], [wr] + XB)
                        P.op("act", lambda e, V_=V_, bkb=bkb: e.activation(out=sig[:, :], in_=ps[bkb][:, :], func=AF.Sigmoid),
                             reads=[("ps", bkb)], writes=["sig"])
                        P.op("act", lambda e, V_=V_, c=c, l=l: e.activation(out=fa[:, 6, 0:30], in_=halo_glu[:, l * 4 + c, :], func=AF.Copy),
                             reads=["hglu"], writes=[("fa", 6)])
                        P.op("dve", lambda e, V_=V_, bka=bka: e.tensor_tensor(out=fa[:, 6, 30:30 + T], in0=ps[bka][:, :], in1=sig[:, :], op=ALU.mult),
                             reads=[("ps", bka), "sig"], writes=[("fa", 6)])
                        P.op("act", lambda e, V_=V_, c=c, l=l: e.activation(out=halo_glu[:, l * 4 + c, :], in_=fa[:, 6, T:T + 30], func=AF.Copy),
                             reads=[("fa", 6)], writes=["hglu"])
                        cw = voff[("conv_w", l)] + c * 31
                        P.op("dve", lambda e, V_=V_, c=c, cw=cw: e.tensor_scalar(out=fa[:, c, 0:T], in0=fa[:, 6, 0:T], scalar1=vecs[:, cw:cw + 1],
                                                                          scalar2=V_("conv_b", c), op0=ALU.mult, op1=ALU.add),
                             reads=[("fa", 6), "vecs"], writes=[("fa", c)])
                        for k in range(1, 31):
                            P.op("dve", lambda e, V_=V_, c=c, k=k, cw=cw: e.scalar_tensor_tensor(
                                out=fa[:, c, 0:T], in0=fa[:, 6, k:k + T], scalar=vecs[:, cw + k:cw + k + 1], in1=fa[:, c, 0:T],
                                op0=ALU.mult, op1=ALU.add), reads=[("fa", 6), ("fa", c), "vecs"], writes=[("fa", c)])
                wv, wr = wnext("big")
                for g in range(4):
                    wwin = 2 ** (g + 1)
                    bk = rot()
                    mm_group(bk, ps[bk][:, :], [(wv[:, kc, g * 128:(g + 1) * 128], U[:, kc, :]) for kc in range(16)], [wr] + XB)
                    P.op("act", lambda e, V_=V_, g=g, l=l: e.activation(out=fa[:, 4, 0:15], in_=halo_pool[:, l * 4 + g, :], func=AF.Copy),
                         reads=["hpool"], writes=[("fa", 4)])
                    P.op("act", lambda e, V_=V_, bk=bk: e.activation(out=fa[:, 4, 15:15 + T], in_=ps[bk][:, :], func=AF.Copy),
                         reads=[("ps", bk)], writes=[("fa", 4)])
                    P.op("act", lambda e, V_=V_, g=g, l=l: e.activation(out=halo_pool[:, l * 4 + g, :], in_=fa[:, 4, T:T + 15], func=AF.Copy),
                         reads=[("fa", 4)], writes=["hpool"])
                    src_u = 4
                    E = T + 15
                    for i in range(g + 1):
                        sh = 2 ** i
                        st_ = 2 ** (i + 1) - 1
                        dst_u = 5 if i % 2 == 0 else 7
                        P.op("dve", lambda e, V_=V_, src_u=src_u, dst_u=dst_u, sh=sh, st_=st_: e.tensor_tensor(
                            out=fa[:, dst_u, st_:E], in0=fa[:, src_u, st_:E], in1=fa[:, src_u, st_ - sh:E - sh], op=ALU.add),
                            reads=[("fa", src_u)], writes=[("fa", dst_u)])
                        src_u = dst_u
                    P.op("dve", lambda e, V_=V_, g=g, src_u=src_u, wwin=wwin: e.scalar_tensor_tensor(
                        out=U[:, 24 + g, :], in0=fa[:, src_u, 15:E], scalar=1.0 / wwin, in1=fa[:, 4, 15:E], op0=ALU.mult, op1=ALU.subtract),
                        reads=[("fa", src_u), ("fa", 4)], writes=[UR(24 + g)])
                    if j == 0:
                        nfix = wwin - 1
                        ic = voff["invcnt"]
                        P.op("dve", lambda e, V_=V_, src_u=src_u, nfix=nfix, ic=ic: e.tensor_tensor(
                            out=sig[:, 0:nfix], in0=fa[:, src_u, 15:15 + nfix], in1=vecs[:, ic:ic + nfix], op=ALU.mult),
                            reads=[("fa", src_u), "vecs"], writes=["sig"])
                        P.op("dve", lambda e, V_=V_, g=g, nfix=nfix: e.tensor_tensor(
                            out=U[:, 24 + g, 0:nfix], in0=sig[:, 0:nfix], in1=fa[:, 4, 15:15 + nfix], op=ALU.subtract),
                            reads=["sig", ("fa", 4)], writes=[UR(24 + g)])
                for g in range(4):
                    bk2 = rot()
                    mm_group(bk2, ps[bk2][:, :], [(wp_view[:, g, :], U[:, 24 + g, :])], [wp_res, UR(24 + g)])
                    P.op("act", lambda e, V_=V_, g=g, bk2=bk2: e.activation(out=U[:, 12 + g, :], in_=ps[bk2][:, :], func=AF.Identity, scale=V_("pscale", g)),
                         reads=[("ps", bk2), "vecs"], writes=[UR(12 + g)])

                set_rot([0, 1, 2, 3])
                wq, wqr = wnext("big")
                wkv, wkvr = wnext("ukv")
                nblk = j + 1
                nkt = nblk * 4
                for h in range(NH):
                    ob, db = (4, 5) if h % 2 == 0 else (6, 7)
                    qn_u = 20 + h % 2
                    qr_u = 22 + h % 2
                    hc = h * 256
                    bk = rot()
                    mm_group(bk, ps[bk][:, :], [(wq[:, kc, hc:hc + 128], U[:, 16 + kc, :]) for kc in range(4)], [wqr] + [UR(16 + k) for k in range(4)])
                    P.op("act", lambda e, V_=V_, bk=bk, qn_u=qn_u: e.activation(out=U[:, qn_u, :], in_=ps[bk][:, :], func=AF.Copy),
                         reads=[("ps", bk)], writes=[UR(qn_u)])
                    bka, bkb = rot(), rot()
                    mm_group(bka, ps[bka][0:64, :], [(wq[:, kc, hc + 128:hc + 192], U[:, 16 + kc, :]) for kc in range(4)], [wqr] + [UR(16 + k) for k in range(4)])
                    mm_group(bkb, ps[bkb][0:64, :], [(wq[:, kc, hc + 192:hc + 256], U[:, 16 + kc, :]) for kc in range(4)], [wqr] + [UR(16 + k) for k in range(4)])
                    rope(bka, bkb, U[0:64, qr_u, :], UR(qr_u))
                    for blk in range(nblk):
                        c0a, c0r = ckv_ap(l, blk, 0)
                        c1a, c1r = ckv_ap(l, blk, 1)
                        bk = rot()
                        mm_group(bk, ps[bk][:, :], [(wkv[:, 0, hc:hc + 128], c0a), (wkv[:, 1, hc:hc + 128], c1a)], [wkvr, c0r, c1r])
                        P.op("act", lambda e, V_=V_, bk=bk, blk=blk: e.activation(out=U[:, 24 + blk, :], in_=ps[bk][:, :], func=AF.Copy),
                             reads=[("ps", bk)], writes=[UR(24 + blk)])
                        bk = rot()

                        def vfn(e, bk=bk, c0a=c0a, c1a=c1a, hc=hc, wkv=wkv):
                            ins = None
                            for sub in range(4):
                                ins = e.matmul(ps[bk][:, sub * 128:(sub + 1) * 128], c0a[:, sub * 128:(sub + 1) * 128], wkv[:, 0, hc + 128:hc + 256], start=True, stop=False)
                                ins = e.matmul(ps[bk][:, sub * 128:(sub + 1) * 128], c1a[:, sub * 128:(sub + 1) * 128], wkv[:, 1, hc + 128:hc + 256], start=False, stop=True)
                            return ins
                        P.op("pe", vfn, reads=[wkvr, c0r, c1r], writes=[("ps", bk)])
                        P.op("dve", lambda e, V_=V_, bk=bk, blk=blk: e.tensor_copy(out=U[:, 28 + blk, :], in_=ps[bk][:, :]),
                             reads=[("ps", bk)], writes=[UR(28 + blk)])
                    pend = None

                    def emit_pv(kt, q0, N, pu, ob=ob, db=db, nkt=nkt):
                        blk, sub = kt // 4, kt % 4

                        def fn(e):
                            e.matmul(ps[ob][:, q0:T], U[:, 28 + blk, sub * 128:(sub + 1) * 128], U[:, pu, 0:N], start=(kt == 0), stop=(kt == nkt - 1))
                            return e.matmul(ps[db][:, q0:T], ones1[:, :], U[:, pu, 0:N], start=(kt == 0), stop=(kt == nkt - 1))
                        P.op("pe", fn, reads=[UR(28 + blk), UR(pu), "ones1"], writes=[("ps", ob), ("ps", db)])

                    for kt in range(nkt):
                        blk, sub = kt // 4, kt % 4
                        r = kt - 4 * j
                        q0 = max(r, 0) * 128
                        N = T - q0
                        pu = 32 + kt % 3
                        kra, krr = kr_ap(l, blk)
                        bk = rot()
                        mm_group(bk, ps[bk][:, 0:N],
                                 [(U[:, 24 + blk, sub * 128:(sub + 1) * 128], U[:, qn_u, q0:T]),
                                  (kra[:, sub * 128:(sub + 1) * 128], U[0:64, qr_u, q0:T])],
                                 [UR(24 + blk), UR(qn_u), UR(qr_u), krr])
                        P.op("act", lambda e, V_=V_, bk=bk, pu=pu, N=N: e.activation(out=U[:, pu, 0:N], in_=ps[bk][:, 0:N], func=AF.Exp, scale=SCALE),
                             reads=[("ps", bk)], writes=[UR(pu)])
                        if r >= 0:
                            P.op("dve", lambda e, V_=V_, pu=pu: e.tensor_tensor(out=U[:, pu, 0:128], in0=U[:, pu, 0:128], in1=tri[:, :], op=ALU.mult),
                                 reads=[UR(pu), "tri"], writes=[UR(pu)])
                        if pend is not None:
                            emit_pv(*pend)
                        pend = (kt, q0, N, pu)
                    emit_pv(*pend)
                    P.op("dve", lambda e, V_=V_, db=db: e.reciprocal(out=rcp, in_=ps[db][:, :]), reads=[("ps", db)], writes=[("lt", 0)])
                    P.op("dve", lambda e, V_=V_, ob=ob, h=h: e.tensor_tensor(out=U[:, h, :], in0=ps[ob][:, :], in1=rcp, op=ALU.mult),
                         reads=[("ps", ob), ("lt", 0)], writes=[UR(h)])

                set_rot([0, 1, 2, 3, 4])
                for c in range(4):
                    P.op("act", lambda e, V_=V_, c=c: e.activation(out=U[:, 35, :], in_=fa[:, c, 0:T], func=AF.Copy), reads=[("fa", c)], writes=[UR(35)])
                    P.op("act", lambda e, V_=V_, c=c: e.activation(out=U[:, 36, :], in_=fa[:, c, 0:T], func=AF.Square), reads=[("fa", c)], writes=[UR(36)])

                    def fn(e, c=c):
                        e.matmul(ps[5][:, :], ones1[:, :], U[:, 35, :], start=(c == 0), stop=(c == 3))
                        return e.matmul(ps[6][:, :], ones1[:, :], U[:, 36, :], start=(c == 0), stop=(c == 3))
                    P.op("pe", fn, reads=[UR(35), UR(36), "ones1"], writes=[("ps", 5), ("ps", 6)])
                P.op("dve", lambda e, V_=V_: e.tensor_scalar(out=mean_sb[:, :], in0=ps[5][:, :], scalar1=1.0 / 512, scalar2=None, op0=ALU.mult),
                     reads=[("ps", 5)], writes=["mean"])
                P.op("dve", lambda e, V_=V_: e.tensor_tensor(out=rcp, in0=mean_sb[:, :], in1=mean_sb[:, :], op=ALU.mult), reads=["mean"], writes=[("lt", 0)])
                P.op("dve", lambda e, V_=V_: e.scalar_tensor_tensor(out=rstd_sb[:, :], in0=ps[6][:, :], scalar=1.0 / 512, in1=rcp, op0=ALU.mult, op1=ALU.subtract),
                     reads=[("ps", 6), ("lt", 0)], writes=["rstd"])
                P.op("act", lambda e, V_=V_: e.activation(out=rstd_sb[:, :], in_=rstd_sb[:, :], func=AF.Sqrt, bias=vcol("eps_ln"), scale=1.0),
                     reads=["rstd", "vecs"], writes=["rstd"])
                P.op("dve", lambda e, V_=V_: e.reciprocal(out=rstd_sb[:, :], in_=rstd_sb[:, :]), reads=["rstd"], writes=["rstd"])
                for c in range(4):
                    i = c % 2
                    P.op("dve", lambda e, V_=V_, c=c, i=i: e.tensor_tensor(out=lt[:, i, :], in0=fa[:, c, 0:T], in1=mean_sb[:, :], op=ALU.subtract),
                         reads=[("fa", c), "mean"], writes=[("lt", i)])
                    P.op("dve", lambda e, V_=V_, i=i: e.tensor_tensor(out=lt[:, i, :], in0=lt[:, i, :], in1=rstd_sb[:, :], op=ALU.mult),
                         reads=[("lt", i), "rstd"], writes=[("lt", i)])
                    P.op("act", lambda e, V_=V_, c=c, i=i: e.activation(out=U[:, 8 + c, :], in_=lt[:, i, :], func=AF.Silu, scale=V_("cln_g", c), bias=V_("cln_b", c)),
                         reads=[("lt", i), "vecs"], writes=[UR(8 + c)])

                for i4 in range(4):
                    wv, wr = wnext("big")
                    for ci in range(4):
                        oc = i4 * 4 + ci
                        bk = rot()
                        mm_group(bk, ps[bk][:, :], [(wv[:, kc, ci * 128:(ci + 1) * 128], U[:, kc, :]) for kc in range(16)], [wr] + XB)
                        P.op("dve", lambda e, V_=V_, oc=oc, bk=bk: e.scalar_tensor_tensor(out=xres[:, oc, :], in0=xres[:, oc, :], scalar=ALPHA, in1=ps[bk][:, :],
                                                                                   op0=ALU.mult, op1=ALU.add),
                             reads=[("xr", oc), ("ps", bk)], writes=[("xr", oc)])
                layer_norm(("ln1_g", l), ("ln1_b", l))

                for half in range(2):
                    for i11 in range(11):
                        wv, wr = wnext("big")
                        for ci in range(2):
                            q = i11 * 2 + ci
                            fc = half * 22 + q
                            accs = []
                            for which in range(2):
                                ch = fc + 44 * which
                                bk = rot()
                                col = ci * 256 + which * 128
                                mm_group(bk, ps[bk][:, :], [(wv[:, kc, col:col + 128], U[:, kc, :]) for kc in range(16)], [wr] + XB)
                                ue = q % 2
                                ue = (2 * q + which) % 2
                                ac = 2 + 2 * which + q % 2
                                hidx = l * 88 + ch
                                fw0 = voff[("fw", l)] + ch * 3
                                P.op("act", lambda e, V_=V_, ue=ue, hidx=hidx: e.activation(out=fa[:, ue, 0:2], in_=halo_up[:, hidx, :], func=AF.Copy),
                                     reads=["hup"], writes=[("fa", ue)])
                                P.op("act", lambda e, V_=V_, ue=ue, bk=bk: e.activation(out=fa[:, ue, 2:2 + T], in_=ps[bk][:, :], func=AF.Copy),
                                     reads=[("ps", bk)], writes=[("fa", ue)])
                                P.op("act", lambda e, V_=V_, ac=ac, bk=bk, fw0=fw0, ch=ch: e.activation(
                                    out=fa[:, ac, 0:T], in_=ps[bk][:, :], func=AF.Identity, scale=vecs[:, fw0 + 2:fw0 + 3], bias=V_("fb", ch)),
                                    reads=[("ps", bk), "vecs"], writes=[("fa", ac)])
                                P.op("act", lambda e, V_=V_, ue=ue, hidx=hidx: e.activation(out=halo_up[:, hidx, :], in_=fa[:, ue, T:T + 2], func=AF.Copy),
                                     reads=[("fa", ue)], writes=["hup"])
                                for k in (1, 0):
                                    P.op("dve", lambda e, V_=V_, ue=ue, ac=ac, k=k, fw0=fw0: e.scalar_tensor_tensor(
                                        out=fa[:, ac, 0:T], in0=fa[:, ue, k:k + T], scalar=vecs[:, fw0 + k:fw0 + k + 1], in1=fa[:, ac, 0:T],
                                        op0=ALU.mult, op1=ALU.add), reads=[("fa", ue), ("fa", ac), "vecs"], writes=[("fa", ac)])
                                accs.append(ac)
                            P.op("act", lambda e, V_=V_, ac=accs[1]: e.activation(out=fa[:, ac, 0:T], in_=fa[:, ac, 0:T], func=AF.Silu),
                                 reads=[("fa", accs[1])], writes=[("fa", accs[1])])
                            P.op("dve", lambda e, V_=V_, q=q, a0=accs[0], a1=accs[1]: e.tensor_tensor(out=U[:, 16 + q, :], in0=fa[:, a0, 0:T], in1=fa[:, a1, 0:T], op=ALU.mult),
                                 reads=[("fa", accs[0]), ("fa", accs[1])], writes=[UR(16 + q)])
                    GR = [UR(16 + q) for q in range(22)]
                    for i8 in range(8):
                        wv, wr = wnext("big")
                        for ci in range(2):
                            oc = i8 * 2 + ci
                            bk = rot()
                            mm_group(bk, ps[bk][:, :], [(wv[:, q, ci * 128:(ci + 1) * 128], U[:, 16 + q, :]) for q in range(22)], [wr] + GR)
                            if half == 0:
                                P.op("dve", lambda e, V_=V_, oc=oc, bk=bk: e.scalar_tensor_tensor(out=xres[:, oc, :], in0=xres[:, oc, :], scalar=ALPHA, in1=ps[bk][:, :],
                                                                                           op0=ALU.mult, op1=ALU.add),
                                     reads=[("xr", oc), ("ps", bk)], writes=[("xr", oc)])
                            else:
                                P.op("dve", lambda e, V_=V_, oc=oc, bk=bk: e.tensor_tensor(out=xres[:, oc, :], in0=xres[:, oc, :], in1=ps[bk][:, :], op=ALU.add),
                                     reads=[("xr", oc), ("ps", bk)], writes=[("xr", oc)])
                layer_norm(("ln2_g", l), ("ln2_b", l))

            P.op("sp", lambda e, b=b, t0=t0: e.dma_start(
                out=outT[b * D:(b + 1) * D, t0:t0 + T].rearrange("(c p) t -> p c t", p=128), in_=xres[:, :, :]),
                reads=[("xr", c) for c in range(16)], dma=True)

    assert wstate["next"] == len(blocks)
    P.finish("sp")
    P.emit(nc)
    es.close()
    return nc


def _pack_vecs(L, inp):
    voff, NV = vec_layout(L)
    v = np.zeros((128, NV), np.float32)

    def put(name, arr):
        o = voff[name]
        v[:, o:o + arr.shape[1]] = arr

    col = lambda a: np.ascontiguousarray(np.asarray(a, np.float32).reshape(-1, 128).T)
    put("ln_in_g", col(inp["ln_in_g"]))
    put("ln_in_b", col(inp["ln_in_b"]))
    inv = (1.0 / (10000.0 ** (np.arange(0, 64, 2, dtype=np.float32) / 64.0))).astype(np.float32)
    fr = np.zeros((128, 1), np.float32)
    fr[0:64, 0] = np.concatenate([inv, inv])
    put("invfreq", fr)
    sg = np.zeros((128, 1), np.float32)
    sg[0:32] = -1.0
    sg[32:64] = 1.0
    put("sinsign", sg)
    put("invcnt", np.tile((1.0 / np.arange(1, 17, dtype=np.float32))[None, :], (128, 1)))
    put("eps_ln", np.full((128, 1), LN_EPS, np.float32))
    put("eps_rms", np.full((128, 1), RMS_EPS, np.float32))
    for l in range(L):
        put(("ln1_g", l), col(inp["ln1_g"][l]))
        put(("ln1_b", l), col(inp["ln1_b"][l]))
        put(("ln2_g", l), col(inp["ln2_g"][l]))
        put(("ln2_b", l), col(inp["ln2_b"][l]))
        put(("qg", l), col(inp["q_norm_g"][l]))
        put(("kvg", l), col(inp["kv_norm_g"][l]))
        cw = np.asarray(inp["conv_w"][l], np.float32)
        put(("conv_w", l), np.ascontiguousarray(cw.T.reshape(4, 128, 31).transpose(1, 0, 2).reshape(128, 124)))
        put(("conv_b", l), col(inp["conv_b"][l]))
        put(("cln_g", l), col(inp["conv_ln_g"][l]))
        put(("cln_b", l), col(inp["conv_ln_b"][l]))
        put(("pscale", l), col(inp["pool_scale"][l]))
        fw = np.asarray(inp["ffn_conv_w"][l], np.float32)
        put(("fw", l), np.ascontiguousarray(fw.T.reshape(88, 128, 3).transpose(1, 0, 2).reshape(128, 264)))
        put(("fb", l), col(inp["ffn_conv_b"][l]))
    return v


def _prep_weights(L, inp):
    f = lambda a: np.asarray(a, np.float32)
    idx_in = list(range(0, 832)) + list(range(800, 832)) + list(range(768, 800))
    for c in range(4):
        idx_in += list(range(832 + c * 128, 832 + (c + 1) * 128)) + list(range(1344 + c * 128, 1344 + (c + 1) * 128))
    idx_in += list(range(1856, 2368))
    idx_in = np.array(idx_in)
    assert idx_in.size == IN_AUG
    w_in = np.ascontiguousarray(f(inp["w_in"])[:L][:, :, idx_in]).reshape(L * D, IN_AUG)
    idx_h = np.array(list(range(0, 192)) + list(range(160, 192)) + list(range(128, 160)))
    w_uq = np.ascontiguousarray(f(inp["w_uq"])[:L][:, :, :, idx_h]).reshape(L * 512, 2048)
    w_ukv = np.ascontiguousarray(f(inp["w_ukv"])[:L]).reshape(L * 256, 2048)
    w_out = np.ascontiguousarray(f(inp["w_out"])[:L]).reshape(L * D, D)
    idx_up = []
    for fc in range(44):
        idx_up += list(range(fc * 128, (fc + 1) * 128)) + list(range(DFF + fc * 128, DFF + (fc + 1) * 128))
    idx_up = np.array(idx_up)
    w_up = np.ascontiguousarray(f(inp["w_up"])[:L][:, :, idx_up]).reshape(L * D, 2 * DFF)
    w_down = np.ascontiguousarray(f(inp["w_down"])[:L]).reshape(L * DFF, D)
    w_pool = np.ascontiguousarray(f(inp["w_pool"])[:L]).reshape(L * 512, 128)
    return dict(w_in=w_in, w_uq=w_uq, w_ukv=w_ukv, w_out=w_out, w_up=w_up, w_down=w_down, w_pool=w_pool)


def run(inp, L, S, n_cores, nseq):
    x = np.asarray(inp["x"], np.float32)
    positions = np.asarray(inp["positions"], np.int32)
    shared = _prep_weights(L, inp)
    shared["vecs"] = _pack_vecs(L, inp)
    shared["tri"] = np.triu(np.ones((128, 128), np.float32))
    nc = build(L, S, nseq)
    in_maps = []
    for cidx in range(n_cores):
        xs = x[cidx * nseq:(cidx + 1) * nseq]
        m = dict(shared)
        m["xT"] = np.ascontiguousarray(xs.transpose(0, 2, 1)).reshape(nseq * D, S)
        m["pos"] = np.ascontiguousarray(positions[cidx * nseq:(cidx + 1) * nseq])
        in_maps.append(m)
    res = run_bass_kernel_spmd(nc, in_maps, core_ids=list(range(n_cores)))
    outs = []
    for cidx in range(n_cores):
        o = res.results[cidx]["outT"].reshape(nseq, D, S).transpose(0, 2, 1)
        outs.append(o)
    return np.ascontiguousarray(np.concatenate(outs, axis=0)).astype(np.float32)


def kernel(**inputs):
    return run(inputs, 4, 2048, 8, 2)
```

```python
import math
import contextlib
import numpy as np
import concourse.bass as bass
import concourse.mybir as mybir
from concourse.bass_utils import run_bass_kernel_spmd

F32 = mybir.dt.float32
BF16 = mybir.dt.bfloat16
I32 = mybir.dt.int32
AF = mybir.ActivationFunctionType
ALU = mybir.AluOpType

ENGS = ["pe", "act", "dve", "pool", "sp"]
D = 2048
T = 512
NH = 8
DFF = 5632
ALPHA = (2.0 * 4) ** 0.25
SCALE = 192 ** -0.5
IN_AUG = 2432
LN_EPS = 1e-5
RMS_EPS = 1e-6


class Prog:
    def __init__(self, n_dma_sems=16):
        self.streams = {e: [] for e in ENGS}
        self.cnt = {}
        self.seen = {e: {} for e in ENGS}
        self.last_w = {}
        self.readers = {}
        self.ndma = n_dma_sems
        self.dma_rr = 0

    def _need(self, eng, deps, key, val):
        if key == "pe" and eng == "pe":
            return
        if self.seen[eng].get(key, 0) >= val:
            return
        if deps.get(key, 0) < val:
            deps[key] = val

    def _need_all(self, eng, deps, lw):
        for (k, v) in lw:
            self._need(eng, deps, k, v)

    def op(self, eng, fn, reads=(), writes=(), dma=False, sw=None):
        psr = [r for r in reads if isinstance(r, tuple) and r[0] == "ps"]
        if psr:
            reads = [r for r in reads if not (isinstance(r, tuple) and r[0] == "ps")]
            writes = list(writes) + [r for r in psr if r not in writes]
        deps = {}
        for r in reads:
            lw = self.last_w.get(r)
            if lw:
                self._need_all(eng, deps, lw)
        for w in writes:
            lw = self.last_w.get(w)
            if lw:
                self._need_all(eng, deps, lw)
            rd = self.readers.get(w)
            if rd:
                for k, v in rd.items():
                    self._need(eng, deps, k, v)
        if sw is not None:
            key = ("sw", sw)
            assert key not in self.cnt
            val = 16
            inc = 16
            self.nsw = max(getattr(self, "nsw", 0), sw + 1)
        elif dma:
            key = ("dma", self.dma_rr)
            self.dma_rr = (self.dma_rr + 1) % self.ndma
            prev = self.cnt.get(key, 0)
            if prev:
                self._need(eng, deps, key, prev)
            val = prev + 16
            inc = 16
        else:
            key = eng
            val = self.cnt.get(eng, 0) + 1
            inc = 1
        self.cnt[key] = val
        st = self.streams[eng]
        for k2, v2 in deps.items():
            st.append(("w", k2, v2))
            self.seen[eng][k2] = v2
        st.append(("o", fn, key, inc))
        for r in reads:
            self.readers.setdefault(r, {})[key] = val
        for w in writes:
            self.last_w[w] = [(key, val)]
            self.readers[w] = {}

    def finish(self, eng="sp"):
        st = self.streams[eng]
        for k, v in self.cnt.items():
            if isinstance(k, tuple) and k[0] == "dma" and self.seen[eng].get(k, 0) < v:
                st.append(("w", k, v))
                self.seen[eng][k] = v

    def emit(self, nc):
        keys = list(ENGS) + [("dma", i) for i in range(self.ndma)] + [("sw", i) for i in range(getattr(self, "nsw", 0))]
        with contextlib.ExitStack() as es:
            sems = {}
            for k in keys:
                nm = k if isinstance(k, str) else "%s%d" % (k[0], k[1])
                sems[k] = es.enter_context(nc.semaphore("s_" + nm))
            block = es.enter_context(nc.Block())

            def run(engobj, stream):
                for it in stream:
                    if it[0] == "w":
                        engobj.wait_ge(sems[it[1]], it[2])
                    elif it[0] == "c":
                        engobj.sem_clear(sems[it[1]]).then_inc(sems["pool"], 1)
                    else:
                        it[1](engobj).then_inc(sems[it[2]], it[3])

            @block.tensor
            def _(e):
                run(e, self.streams["pe"])

            @block.scalar
            def _(e):
                run(e, self.streams["act"])

            @block.vector
            def _(e):
                run(e, self.streams["dve"])

            @block.gpsimd
            def _(e):
                run(e, self.streams["pool"])

            @block.sync
            def _(e):
                run(e, self.streams["sp"])


def vec_layout(L):
    off = {}
    n = 0

    def add(name, w):
        nonlocal n
        off[name] = n
        n += w

    add("ln_in_g", 16)
    add("ln_in_b", 16)
    add("invfreq", 1)
    add("sinsign", 1)
    add("invcnt", 16)
    add("eps_ln", 1)
    add("eps_rms", 1)
    for l in range(L):
        for nm, w in [("ln1_g", 16), ("ln1_b", 16), ("ln2_g", 16), ("ln2_b", 16), ("qg", 4), ("kvg", 2),
                      ("conv_w", 124), ("conv_b", 4), ("cln_g", 4), ("cln_b", 4), ("pscale", 4),
                      ("fw", 264), ("fb", 88)]:
            add((nm, l), w)
    return off, n


def build(L, S, NSEQ):
    NT = S // T
    voff, NV = vec_layout(L)
    nc = bass.Bass("TRN2", target_bir_lowering=False)

    def din(name, shape, dt=F32):
        return nc.dram_tensor(name, shape, dt, kind="ExternalInput").ap()

    xT = din("xT", [NSEQ * D, S])
    pos = din("pos", [NSEQ, S], I32)
    w_in = din("w_in", [L * D, IN_AUG])
    w_uq = din("w_uq", [L * 512, 2048])
    w_ukv = din("w_ukv", [L * 256, 2048])
    w_out = din("w_out", [L * D, D])
    w_up = din("w_up", [L * D, 2 * DFF])
    w_down = din("w_down", [L * DFF, D])
    w_pool = din("w_pool", [L * 512, 128])
    vecs_d = din("vecs", [128, NV])
    tri_d = din("tri", [128, 128])
    outT = nc.dram_tensor("outT", [NSEQ * D, S], F32, kind="ExternalOutput").ap()

    P = Prog()
    es = contextlib.ExitStack()
    wsrc = {"w_in": w_in, "w_uq": w_uq, "w_ukv": w_ukv, "w_out": w_out, "w_up": w_up, "w_down": w_down, "w_pool": w_pool}
    wrows = {"w_in": D, "w_uq": 512, "w_ukv": 256, "w_out": D, "w_up": D, "w_down": DFF, "w_pool": 512}
    wbf = {}
    for nm, ap_ in wsrc.items():
        wbf[nm] = nc.dram_tensor(nm + "_bf", list(ap_.shape), BF16).ap()
    ncv = 0
    for l in range(L):
        for nm in ["w_pool", "w_in", "w_uq", "w_ukv", "w_out", "w_up", "w_down"]:
            r0, r1 = l * wrows[nm], (l + 1) * wrows[nm]
            P.op("pool", lambda e, nm=nm, r0=r0, r1=r1: e.dma_start(out=wbf[nm][r0:r1, :], in_=wsrc[nm][r0:r1, :], max_dma_last_dim=8192),
                 writes=[("wbf", nm, l)], sw=ncv)
            ncv += 1

    def sb(name, shape, dt):
        return es.enter_context(nc.sbuf_tensor(name, shape, dt))

    xres = sb("xres", [128, 16, T], F32)
    U = sb("U", [128, 40, T], BF16)
    cache_ckv = [sb("cckv%d" % l, [128, 2, 3 * T], BF16) for l in range(L)]
    cache_kr = [sb("ckr%d" % l, [64, 3 * T], BF16) for l in range(L)]
    kr_cur = sb("kr_cur", [64, T], BF16)
    slots = [sb("ws0", [128, 8192], BF16), sb("ws1", [128, 8192], BF16), sb("ws2", [128, 4096], BF16)]
    wpool_sb = sb("wpool", [128, 4, 128], BF16)
    fa = sb("fa", [128, 8, 544], F32)
    sig = sb("sig", [128, T], F32)
    rt = sb("rt", [128, 2, T], F32)
    mean_sb = sb("mean_sb", [128, T], F32)
    rstd_sb = sb("rstd_sb", [128, T], F32)
    lt = sb("lt", [128, 2, T], F32)
    rcp = lt[:, 0, :]
    cos_t = sb("cos_t", [64, T], F32)
    sin_t = sb("sin_t", [64, T], F32)
    posi = sb("posi", [64, T], I32)
    zb = sb("zb", [128, 4, T], BF16)
    vecs = sb("vecs_sb", [128, NV], F32)
    tri_f = sb("tri_f", [128, 128], F32)
    tri = sb("tri_b", [128, 128], BF16)
    ones1 = sb("ones1", [128, 128], BF16)
    onesN = sb("onesN", [128, 128], BF16)
    halo_glu = sb("halo_glu", [128, L * 4, 30], F32)
    halo_pool = sb("halo_pool", [128, L * 4, 15], F32)
    halo_up = sb("halo_up", [128, L * 88, 2], F32)
    ps = [es.enter_context(nc.psum_tensor("ps%d" % i, [128, T], F32)) for i in range(8)]

    def vcol(name, c=0, n=1):
        o = voff[name] + c
        return vecs[:, o:o + n]

    rot_state = {"set": [0, 1, 2, 3, 4], "i": 0}

    def rot():
        s = rot_state["set"]
        b = s[rot_state["i"] % len(s)]
        rot_state["i"] += 1
        return b

    def set_rot(s):
        rot_state["set"] = s
        rot_state["i"] = 0

    def mm_group(bank, out_ap, pairs, reads):
        def fn(e, pairs=pairs, out_ap=out_ap):
            n = len(pairs)
            ins = None
            for i, (l_, r_) in enumerate(pairs):
                ins = e.matmul(out_ap, l_, r_, start=(i == 0), stop=(i == n - 1))
            return ins
        P.op("pe", fn, reads=reads, writes=[("ps", bank)])

    UR = lambda i: ("u", i)

    def mm_kouter(banks, wv, cols, wr):
        for kc in range(16):
            def fn(e, kc=kc, banks=banks, wv=wv, cols=cols):
                ins = None
                for bk, c0 in zip(banks, cols):
                    ins = e.matmul(ps[bk][:, :], wv[:, kc, c0:c0 + 128], U[:, kc, :], start=(kc == 0), stop=(kc == 15))
                return ins
            P.op("pe", fn, reads=[wr, UR(kc)], writes=[("ps", bk) for bk in banks])

    def gen_blocks():
        for b in range(NSEQ):
            for j in range(NT):
                for l in range(L):
                    yield ("wp", "w_pool", l * 512, 4, 0, 128)
                    for (c0, C) in [(0, 512), (512, 384), (896, 512), (1408, 512), (1920, 512)]:
                        yield ("big", "w_in", l * D, 16, c0, C)
                    yield ("big", "w_uq", l * 512, 4, 0, 2048)
                    yield ("ukv", "w_ukv", l * 256, 2, 0, 2048)
                    for i in range(4):
                        yield ("big", "w_out", l * D, 16, i * 512, 512)
                    for half in range(2):
                        for i in range(11):
                            yield ("big", "w_up", l * D, 16, (half * 11 + i) * 512, 512)
                        for i in range(8):
                            yield ("big", "w_down", l * DFF + half * 2816, 22, i * 256, 256)

    blocks = list(gen_blocks())
    wstate = {"next": 0, "issued": 0, "big": 0}
    wviews = {}

    def w_issue(i):
        kind, wname, r0, KC, c0, C = blocks[i]
        dram2d = wbf[wname]
        lyr = r0 // wrows[wname]
        if kind == "big":
            s = wstate["big"] % 2
            wstate["big"] += 1
            res = ("ws", s)
            view = slots[s][:, 0:KC * C].rearrange("p (kc c) -> p kc c", kc=KC)
        elif kind == "ukv":
            res = ("ws", 2)
            view = slots[2][:, 0:KC * C].rearrange("p (kc c) -> p kc c", kc=KC)
        else:
            res = "wpool"
            view = wpool_sb[:, :, :]
        src = dram2d[r0:r0 + KC * 128, c0:c0 + C].rearrange("(kc p) c -> p kc c", p=128)
        P.op("sp", lambda e, view=view, src=src: e.dma_start(out=view, in_=src), reads=[("wbf", wname, lyr)], writes=[res], dma=True)
        wviews[i] = (view, res)

    def wnext(kind):
        i = wstate["next"]
        assert blocks[i][0] == kind, (blocks[i][0], kind)
        wstate["next"] += 1
        lim = i
        nb = 0
        k = i + 1
        while k < len(blocks) and nb < 1:
            if blocks[k][0] == "big":
                nb += 1
            lim = k
            k += 1
        while wstate["issued"] <= lim:
            w_issue(wstate["issued"])
            wstate["issued"] += 1
        return wviews.pop(i)

    def ckv_ap(l, blk, kc):
        if blk < 3:
            return cache_ckv[l][:, kc, blk * T:(blk + 1) * T], ("ckv", l, blk, kc)
        return U[:, 38 + kc, :], UR(38 + kc)

    def kr_ap(l, blk):
        if blk < 3:
            return cache_kr[l][:, blk * T:(blk + 1) * T], ("kr", l, blk)
        return kr_cur[:, :], "kr_cur"

    P.op("sp", lambda e: e.dma_start(out=vecs[:, :], in_=vecs_d), writes=["vecs"], dma=True)
    P.op("sp", lambda e: e.dma_start(out=tri_f[:, :], in_=tri_d), writes=["tri_f"], dma=True)
    P.op("act", lambda e: e.activation(out=tri[:, :], in_=tri_f[:, :], func=AF.Copy), reads=["tri_f"], writes=["tri"])
    P.op("dve", lambda e: e.memset(ones1[:, :], 1.0), writes=["ones1"])
    P.op("dve", lambda e: e.memset(onesN[:, :], 1.0 / D), writes=["onesN"])

    def layer_norm(gname, bname):
        xr_all = [("xr", c) for c in range(16)]
        for c in range(16):
            i = c % 2
            P.op("act", lambda e, c=c, i=i: e.activation(out=zb[:, i, :], in_=xres[:, c, :], func=AF.Copy),
                 reads=[("xr", c)], writes=[("zb", i)])
            P.op("act", lambda e, c=c, i=i: e.activation(out=zb[:, 2 + i, :], in_=xres[:, c, :], func=AF.Square),
                 reads=[("xr", c)], writes=[("zb", 2 + i)])

            def fn(e, c=c, i=i):
                e.matmul(ps[5][:, :], onesN[:, :], zb[:, i, :], start=(c == 0), stop=(c == 15))
                return e.matmul(ps[6][:, :], onesN[:, :], zb[:, 2 + i, :], start=(c == 0), stop=(c == 15))
            P.op("pe", fn, reads=[("zb", i), ("zb", 2 + i), "onesN"], writes=[("ps", 5), ("ps", 6)])
        stats_finish(5, 6)
        for c in range(16):
            i = c % 2
            P.op("pool", lambda e, c=c, i=i: e.tensor_tensor(out=lt[:, i, :], in0=xres[:, c, :], in1=mean_sb[:, :], op=ALU.subtract),
                 reads=[("xr", c), "mean"], writes=[("lt", i)])
            P.op("dve", lambda e, i=i: e.tensor_tensor(out=lt[:, i, :], in0=lt[:, i, :], in1=rstd_sb[:, :], op=ALU.mult),
                 reads=[("lt", i), "rstd"], writes=[("lt", i)])
            P.op("act", lambda e, c=c, i=i: e.activation(out=xres[:, c, :], in_=lt[:, i, :], func=AF.Identity,
                                                         scale=vcol(gname, c), bias=vcol(bname, c)),
                 reads=[("lt", i), "vecs"], writes=[("xr", c)])
            if c % 2 == 0:
                P.op("act", lambda e, c=c, i=i: e.activation(out=U[:, c, :], in_=lt[:, i, :], func=AF.Identity,
                                                             scale=vcol(gname, c), bias=vcol(bname, c)),
                     reads=[("lt", i), "vecs"], writes=[UR(c)])
            else:
                P.op("dve", lambda e, c=c, i=i: e.tensor_scalar(out=U[:, c, :], in0=lt[:, i, :], scalar1=vcol(gname, c), scalar2=vcol(bname, c),
                                                                op0=ALU.mult, op1=ALU.add),
                     reads=[("lt", i), "vecs"], writes=[UR(c)])

    def stats_finish(bm, bq):
        P.op("dve", lambda e: e.tensor_copy(out=mean_sb[:, :], in_=ps[bm][:, :]), reads=[("ps", bm)], writes=["mean"])
        P.op("dve", lambda e: e.tensor_tensor(out=rcp, in0=mean_sb[:, :], in1=mean_sb[:, :], op=ALU.mult),
             reads=["mean"], writes=[("lt", 0)])
        P.op("dve", lambda e: e.tensor_tensor(out=rstd_sb[:, :], in0=ps[bq][:, :], in1=rcp, op=ALU.subtract),
             reads=[("ps", bq), ("lt", 0)], writes=["rstd"])
        P.op("act", lambda e: e.activation(out=rstd_sb[:, :], in_=rstd_sb[:, :], func=AF.Sqrt, bias=vcol("eps_ln"), scale=1.0),
             reads=["rstd", "vecs"], writes=["rstd"])
        P.op("dve", lambda e: e.reciprocal(out=rstd_sb[:, :], in_=rstd_sb[:, :]), reads=["rstd"], writes=["rstd"])

    def range_reduce(buf, res):
        t1 = rt[0:64, 1, :]
        P.op("dve", lambda e: e.tensor_scalar(out=t1, in0=buf, scalar1=1.0 / (2 * math.pi), scalar2=None, op0=ALU.mult),
             reads=[res], writes=[("rt", 1)])
        P.op("dve", lambda e: e.tensor_copy(out=posi[:, :], in_=t1), reads=[("rt", 1)], writes=["posi"])
        P.op("dve", lambda e: e.tensor_copy(out=t1, in_=posi[:, :]), reads=["posi"], writes=[("rt", 1)])
        P.op("dve", lambda e: e.scalar_tensor_tensor(out=buf, in0=t1, scalar=-2 * math.pi, in1=buf, op0=ALU.mult, op1=ALU.add),
             reads=[("rt", 1), res], writes=[res])
        P.op("dve", lambda e: e.tensor_scalar(out=t1, in0=buf, scalar1=math.pi, scalar2=2 * math.pi, op0=ALU.is_gt, op1=ALU.mult),
             reads=[res], writes=[("rt", 1)])
        P.op("dve", lambda e: e.tensor_tensor(out=buf, in0=buf, in1=t1, op=ALU.subtract), reads=[res, ("rt", 1)], writes=[res])
        P.op("dve", lambda e: e.tensor_scalar(out=t1, in0=buf, scalar1=-math.pi, scalar2=2 * math.pi, op0=ALU.is_lt, op1=ALU.mult),
             reads=[res], writes=[("rt", 1)])
        P.op("dve", lambda e: e.tensor_tensor(out=buf, in0=buf, in1=t1, op=ALU.add), reads=[res, ("rt", 1)], writes=[res])

    def rope(ps_a, ps_b, dst, dst_res):
        P.op("dve", lambda e: e.tensor_tensor(out=rt[0:64, 0, :], in0=ps[ps_a][0:64, :], in1=cos_t[:, :], op=ALU.mult),
             reads=[("ps", ps_a), "cos"], writes=[("rt", 0)])
        P.op("dve", lambda e: e.tensor_tensor(out=rt[0:64, 1, :], in0=ps[ps_b][0:64, :], in1=sin_t[:, :], op=ALU.mult),
             reads=[("ps", ps_b), "sin"], writes=[("rt", 1)])
        P.op("dve", lambda e: e.tensor_tensor(out=dst, in0=rt[0:64, 0, :], in1=rt[0:64, 1, :], op=ALU.add),
             reads=[("rt", 0), ("rt", 1)], writes=[dst_res])

    XB = [UR(c) for c in range(16)]

    for b in range(NSEQ):
        for j in range(NT):
            t0 = j * T
            P.op("sp", lambda e, b=b, t0=t0: e.dma_start(
                out=xres[:, :, :], in_=xT[b * D:(b + 1) * D, t0:t0 + T].rearrange("(c p) t -> p c t", p=128)),
                writes=[("xr", c) for c in range(16)], dma=True)
            P.op("sp", lambda e, b=b, t0=t0: e.dma_start(out=posi[:, :], in_=pos[b:b + 1, t0:t0 + T].partition_broadcast(64)),
                 writes=["posi"], dma=True)
            if j == 0:
                P.op("dve", lambda e: e.memset(halo_glu[:, :, :], 0.0), writes=["hglu"])
                P.op("dve", lambda e: e.memset(halo_pool[:, :, :], 0.0), writes=["hpool"])
                P.op("dve", lambda e: e.memset(halo_up[:, :, :], 0.0), writes=["hup"])
            P.op("dve", lambda e: e.tensor_copy(out=sin_t[:, :], in_=posi[:, :]), reads=["posi"], writes=["sin"])
            P.op("dve", lambda e: e.tensor_scalar(out=sin_t[:, :], in0=sin_t[:, :], scalar1=vecs[0:64, voff["invfreq"]:voff["invfreq"] + 1],
                                                  scalar2=None, op0=ALU.mult), reads=["sin", "vecs"], writes=["sin"])
            P.op("dve", lambda e: e.tensor_scalar(out=cos_t[:, :], in0=sin_t[:, :], scalar1=math.pi / 2, scalar2=None, op0=ALU.add),
                 reads=["sin"], writes=["cos"])
            range_reduce(cos_t[:, :], "cos")
            P.op("act", lambda e: e.activation(out=cos_t[:, :], in_=cos_t[:, :], func=AF.Sin), reads=["cos"], writes=["cos"])
            range_reduce(sin_t[:, :], "sin")
            P.op("act", lambda e: e.activation(out=sin_t[:, :], in_=sin_t[:, :], func=AF.Sin), reads=["sin"], writes=["sin"])
            P.op("dve", lambda e: e.tensor_scalar(out=sin_t[:, :], in0=sin_t[:, :], scalar1=vecs[0:64, voff["sinsign"]:voff["sinsign"] + 1],
                                                  scalar2=None, op0=ALU.mult), reads=["sin", "vecs"], writes=["sin"])
            set_rot([0, 1, 2, 3, 4])
            layer_norm("ln_in_g", "ln_in_b")

            for l in range(L):
                V_ = lambda nm, c=0, l=l: vcol((nm, l), c)
                set_rot([0, 1, 2, 3, 4])
                wp_view, wp_res = wnext("wp")
                wv, wr = wnext("big")
                b0banks = [rot() for _ in range(4)]
                mm_kouter(b0banks, wv, [c * 128 for c in range(4)], wr)
                for c in range(4):
                    bk = b0banks[c]
                    P.op("act", lambda e, V_=V_, c=c, bk=bk: e.activation(out=fa[:, c, 0:T], in_=ps[bk][:, :], func=AF.Copy),
                         reads=[("ps", bk)], writes=[("fa", c)])
                    P.op("act", lambda e, V_=V_, c=c, bk=bk: e.activation(out=U[:, 35 + c % 2, :], in_=ps[bk][:, :], func=AF.Square),
                         reads=[("ps", bk)], writes=[UR(35 + c % 2)])
                    P.op("pe", lambda e, V_=V_, c=c: e.matmul(ps[7][:, :], ones1[:, :], U[:, 35 + c % 2, :], start=(c == 0), stop=(c == 3)),
                         reads=[UR(35 + c % 2), "ones1"], writes=[("ps", 7)])
                P.op("act", lambda e, V_=V_: e.activation(out=rstd_sb[:, :], in_=ps[7][:, :], func=AF.Sqrt, bias=vcol("eps_rms"), scale=1.0 / 512),
                     reads=[("ps", 7), "vecs"], writes=["rstd"])
                P.op("dve", lambda e, V_=V_: e.reciprocal(out=rstd_sb[:, :], in_=rstd_sb[:, :]), reads=["rstd"], writes=["rstd"])
                for c in range(4):
                    P.op("dve", lambda e, V_=V_, c=c: e.scalar_tensor_tensor(out=U[:, 16 + c, :], in0=fa[:, c, 0:T], scalar=V_("qg", c),
                                                                      in1=rstd_sb[:, :], op0=ALU.mult, op1=ALU.mult),
                         reads=[("fa", c), "rstd", "vecs"], writes=[UR(16 + c)])
                wv, wr = wnext("big")
                for c in range(2):
                    bk = rot()
                    mm_group(bk, ps[bk][:, :], [(wv[:, kc, c * 128:(c + 1) * 128], U[:, kc, :]) for kc in range(16)], [wr] + XB)
                    P.op("act", lambda e, V_=V_, c=c, bk=bk: e.activation(out=fa[:, 4 + c, 0:T], in_=ps[bk][:, :], func=AF.Copy),
                         reads=[("ps", bk)], writes=[("fa", 4 + c)])
                    P.op("act", lambda e, V_=V_, c=c, bk=bk: e.activation(out=U[:, 35 + c % 2, :], in_=ps[bk][:, :], func=AF.Square),
                         reads=[("ps", bk)], writes=[UR(35 + c % 2)])
                    P.op("pe", lambda e, V_=V_, c=c: e.matmul(ps[7][:, :], ones1[:, :], U[:, 35 + c % 2, :], start=(c == 0), stop=(c == 1)),
                         reads=[UR(35 + c % 2), "ones1"], writes=[("ps", 7)])
                P.op("act", lambda e, V_=V_: e.activation(out=mean_sb[:, :], in_=ps[7][:, :], func=AF.Sqrt, bias=vcol("eps_rms"), scale=1.0 / 256),
                     reads=[("ps", 7), "vecs"], writes=["mean"])
                P.op("dve", lambda e, V_=V_: e.reciprocal(out=mean_sb[:, :], in_=mean_sb[:, :]), reads=["mean"], writes=["mean"])
                for c in range(2):
                    dst, dres = ckv_ap(l, j, c)
                    P.op("dve", lambda e, V_=V_, c=c, dst=dst: e.scalar_tensor_tensor(out=dst, in0=fa[:, 4 + c, 0:T], scalar=V_("kvg", c),
                                                                               in1=mean_sb[:, :], op0=ALU.mult, op1=ALU.mult),
                         reads=[("fa", 4 + c), "mean", "vecs"], writes=[dres])
                bka, bkb = rot(), rot()
                mm_group(bka, ps[bka][0:64, :], [(wv[:, kc, 256:320], U[:, kc, :]) for kc in range(16)], [wr] + XB)
                mm_group(bkb, ps[bkb][0:64, :], [(wv[:, kc, 320:384], U[:, kc, :]) for kc in range(16)], [wr] + XB)
                dst, dres = kr_ap(l, j)
                rope(bka, bkb, dst, dres)
                for blk2 in range(2):
                    wv, wr = wnext("big")
                    for ci in range(2):
                        c = blk2 * 2 + ci
                        bka, bkb = rot(), rot()
                        mm_group(bka, ps[bka][:, :], [(wv[:, kc, ci * 256:ci * 256 + 128], U[:, kc, :]) for kc in range(16)], [wr] + XB)
                        mm_group(bkb, ps[bkb][:, :], [(wv[:, kc, ci * 256 + 128:ci * 256 + 256], U[:, kc, :]) for kc in range(16)], [wr] + XB)
                        P.op("act", lambda e, V_=V_, bkb=bkb: e.activation(out=sig[:, :], in_=ps[bkb][:, :], func=AF.Sigmoid),
                             reads=[("ps", bkb)], writes=["sig"])
                        P.op("act", lambda e, V_=V_, c=c, l=l: e.activation(out=fa[:, 6, 0:30], in_=halo_glu[:, l * 4 + c, :], func=AF.Copy),
                             reads=["hglu"], writes=[("fa", 6)])
                        P.op("dve", lambda e, V_=V_, bka=bka: e.tensor_tensor(out=fa[:, 6, 30:30 + T], in0=ps[bka][:, :], in1=sig[:, :], op=ALU.mult),
                             reads=[("ps", bka), "sig"], writes=[("fa", 6)])
                        P.op("act", lambda e, V_=V_, c=c, l=l: e.activation(out=halo_glu[:, l * 4 + c, :], in_=fa[:, 6, T:T + 30], func=AF.Copy),
                             reads=[("fa", 6)], writes=["hglu"])
                        cw = voff[("conv_w", l)] + c * 31
                        P.op("dve", lambda e, V_=V_, c=c, cw=cw: e.tensor_scalar(out=fa[:, c, 0:T], in0=fa[:, 6, 0:T], scalar1=vecs[:, cw:cw + 1],
                                                                          scalar2=V_("conv_b", c), op0=ALU.mult, op1=ALU.add),
                             reads=[("fa", 6), "vecs"], writes=[("fa", c)])
                        for k in range(1, 31):
                            P.op("dve", lambda e, V_=V_, c=c, k=k, cw=cw: e.scalar_tensor_tensor(
                                out=fa[:, c, 0:T], in0=fa[:, 6, k:k + T], scalar=vecs[:, cw + k:cw + k + 1], in1=fa[:, c, 0:T],
                                op0=ALU.mult, op1=ALU.add), reads=[("fa", 6), ("fa", c), "vecs"], writes=[("fa", c)])
                wv, wr = wnext("big")
                for g in range(4):
                    wwin = 2 ** (g + 1)
                    bk = rot()
                    mm_group(bk, ps[bk][:, :], [(wv[:, kc, g * 128:(g + 1) * 128], U[:, kc, :]) for kc in range(16)], [wr] + XB)
                    P.op("act", lambda e, V_=V_, g=g, l=l: e.activation(out=fa[:, 4, 0:15], in_=halo_pool[:, l * 4 + g, :], func=AF.Copy),
                         reads=["hpool"], writes=[("fa", 4)])
                    P.op("act", lambda e, V_=V_, bk=bk: e.activation(out=fa[:, 4, 15:15 + T], in_=ps[bk][:, :], func=AF.Copy),
                         reads=[("ps", bk)], writes=[("fa", 4)])
                    P.op("act", lambda e, V_=V_, g=g, l=l: e.activation(out=halo_pool[:, l * 4 + g, :], in_=fa[:, 4, T:T + 15], func=AF.Copy),
                         reads=[("fa", 4)], writes=["hpool"])
                    src_u = 4
                    E = T + 15
                    for i in range(g + 1):
                        sh = 2 ** i
                        st_ = 2 ** (i + 1) - 1
                        dst_u = 5 if i % 2 == 0 else 7
                        P.op("dve", lambda e, V_=V_, src_u=src_u, dst_u=dst_u, sh=sh, st_=st_: e.tensor_tensor(
                            out=fa[:, dst_u, st_:E], in0=fa[:, src_u, st_:E], in1=fa[:, src_u, st_ - sh:E - sh], op=ALU.add),
                            reads=[("fa", src_u)], writes=[("fa", dst_u)])
                        src_u = dst_u
                    P.op("dve", lambda e, V_=V_, g=g, src_u=src_u, wwin=wwin: e.scalar_tensor_tensor(
                        out=U[:, 24 + g, :], in0=fa[:, src_u, 15:E], scalar=1.0 / wwin, in1=fa[:, 4, 15:E], op0=ALU.mult, op1=ALU.subtract),
                        reads=[("fa", src_u), ("fa", 4)], writes=[UR(24 + g)])
                    if j == 0:
                        nfix = wwin - 1
                        ic = voff["invcnt"]
                        P.op("dve", lambda e, V_=V_, src_u=src_u, nfix=nfix, ic=ic: e.tensor_tensor(
                            out=sig[:, 0:nfix], in0=fa[:, src_u, 15:15 + nfix], in1=vecs[:, ic:ic + nfix], op=ALU.mult),
                            reads=[("fa", src_u), "vecs"], writes=["sig"])
                        P.op("dve", lambda e, V_=V_, g=g, nfix=nfix: e.tensor_tensor(
                            out=U[:, 24 + g, 0:nfix], in0=sig[:, 0:nfix], in1=fa[:, 4, 15:15 + nfix], op=ALU.subtract),
                            reads=["sig", ("fa", 4)], writes=[UR(24 + g)])
                for g in range(4):
                    bk2 = rot()
                    mm_group(bk2, ps[bk2][:, :], [(wp_view[:, g, :], U[:, 24 + g, :])], [wp_res, UR(24 + g)])
                    P.op("act", lambda e, V_=V_, g=g, bk2=bk2: e.activation(out=U[:, 12 + g, :], in_=ps[bk2][:, :], func=AF.Identity, scale=V_("pscale", g)),
                         reads=[("ps", bk2), "vecs"], writes=[UR(12 + g)])

                set_rot([0, 1, 2, 3])
                wq, wqr = wnext("big")
                wkv, wkvr = wnext("ukv")
                nblk = j + 1
                nkt = nblk * 4
                for h in range(NH):
                    ob, db = (4, 5) if h % 2 == 0 else (6, 7)
                    qn_u = 20 + h % 2
                    qr_u = 22 + h % 2
                    hc = h * 256
                    bk = rot()
                    mm_group(bk, ps[bk][:, :], [(wq[:, kc, hc:hc + 128], U[:, 16 + kc, :]) for kc in range(4)], [wqr] + [UR(16 + k) for k in range(4)])
                    P.op("act", lambda e, V_=V_, bk=bk, qn_u=qn_u: e.activation(out=U[:, qn_u, :], in_=ps[bk][:, :], func=AF.Copy),
                         reads=[("ps", bk)], writes=[UR(qn_u)])
                    bka, bkb = rot(), rot()
                    mm_group(bka, ps[bka][0:64, :], [(wq[:, kc, hc + 128:hc + 192], U[:, 16 + kc, :]) for kc in range(4)], [wqr] + [UR(16 + k) for k in range(4)])
                    mm_group(bkb, ps[bkb][0:64, :], [(wq[:, kc, hc + 192:hc + 256], U[:, 16 + kc, :]) for kc in range(4)], [wqr] + [UR(16 + k) for k in range(4)])
                    rope(bka, bkb, U[0:64, qr_u, :], UR(qr_u))
                    for blk in range(nblk):
                        c0a, c0r = ckv_ap(l, blk, 0)
                        c1a, c1r = ckv_ap(l, blk, 1)
                        bk = rot()
                        mm_group(bk, ps[bk][:, :], [(wkv[:, 0, hc:hc + 128], c0a), (wkv[:, 1, hc:hc + 128], c1a)], [wkvr, c0r, c1r])
                        P.op("act", lambda e, V_=V_, bk=bk, blk=blk: e.activation(out=U[:, 24 + blk, :], in_=ps[bk][:, :], func=AF.Copy),
                             reads=[("ps", bk)], writes=[UR(24 + blk)])
                        bk = rot()

                        def vfn(e, bk=bk, c0a=c0a, c1a=c1a, hc=hc, wkv=wkv):
                            ins = None
                            for sub in range(4):
                                ins = e.matmul(ps[bk][:, sub * 128:(sub + 1) * 128], c0a[:, sub * 128:(sub + 1) * 128], wkv[:, 0, hc + 128:hc + 256], start=True, stop=False)
                                ins = e.matmul(ps[bk][:, sub * 128:(sub + 1) * 128], c1a[:, sub * 128:(sub + 1) * 128], wkv[:, 1, hc + 128:hc + 256], start=False, stop=True)
                            return ins
                        P.op("pe", vfn, reads=[wkvr, c0r, c1r], writes=[("ps", bk)])
                        P.op("dve", lambda e, V_=V_, bk=bk, blk=blk: e.tensor_copy(out=U[:, 28 + blk, :], in_=ps[bk][:, :]),
                             reads=[("ps", bk)], writes=[UR(28 + blk)])
                    pend = None

                    def emit_pv(kt, q0, N, pu, ob=ob, db=db, nkt=nkt):
                        blk, sub = kt // 4, kt % 4

                        def fn(e):
                            e.matmul(ps[ob][:, q0:T], U[:, 28 + blk, sub * 128:(sub + 1) * 128], U[:, pu, 0:N], start=(kt == 0), stop=(kt == nkt - 1))
                            return e.matmul(ps[db][:, q0:T], ones1[:, :], U[:, pu, 0:N], start=(kt == 0), stop=(kt == nkt - 1))
                        P.op("pe", fn, reads=[UR(28 + blk), UR(pu), "ones1"], writes=[("ps", ob), ("ps", db)])

                    for kt in range(nkt):
                        blk, sub = kt // 4, kt % 4
                        r = kt - 4 * j
                        q0 = max(r, 0) * 128
                        N = T - q0
                        pu = 32 + kt % 3
                        kra, krr = kr_ap(l, blk)
                        bk = rot()
                        mm_group(bk, ps[bk][:, 0:N],
                                 [(U[:, 24 + blk, sub * 128:(sub + 1) * 128], U[:, qn_u, q0:T]),
                                  (kra[:, sub * 128:(sub + 1) * 128], U[0:64, qr_u, q0:T])],
                                 [UR(24 + blk), UR(qn_u), UR(qr_u), krr])
                        P.op("act", lambda e, V_=V_, bk=bk, pu=pu, N=N: e.activation(out=U[:, pu, 0:N], in_=ps[bk][:, 0:N], func=AF.Exp, scale=SCALE),
                             reads=[("ps", bk)], writes=[UR(pu)])
                        if r >= 0:
                            P.op("dve", lambda e, V_=V_, pu=pu: e.tensor_tensor(out=U[:, pu, 0:128], in0=U[:, pu, 0:128], in1=tri[:, :], op=ALU.mult),
                                 reads=[UR(pu), "tri"], writes=[UR(pu)])
                        if pend is not None:
                            emit_pv(*pend)
                        pend = (kt, q0, N, pu)
                    emit_pv(*pend)
                    P.op("dve", lambda e, V_=V_, db=db: e.reciprocal(out=rcp, in_=ps[db][:, :]), reads=[("ps", db)], writes=[("lt", 0)])
                    P.op("dve", lambda e, V_=V_, ob=ob, h=h: e.tensor_tensor(out=U[:, h, :], in0=ps[ob][:, :], in1=rcp, op=ALU.mult),
                         reads=[("ps", ob), ("lt", 0)], writes=[UR(h)])
                    if h == 1:
                        cbm, cbq = rot(), rot()
                        for c in range(4):
                            P.op("act", lambda e, V_=V_, c=c: e.activation(out=U[:, 35, :], in_=fa[:, c, 0:T], func=AF.Copy), reads=[("fa", c)], writes=[UR(35)])
                            P.op("act", lambda e, V_=V_, c=c: e.activation(out=U[:, 36, :], in_=fa[:, c, 0:T], func=AF.Square), reads=[("fa", c)], writes=[UR(36)])

                            def fn(e, c=c, cbm=cbm, cbq=cbq):
                                e.matmul(ps[cbm][:, :], ones1[:, :], U[:, 35, :], start=(c == 0), stop=(c == 3))
                                return e.matmul(ps[cbq][:, :], ones1[:, :], U[:, 36, :], start=(c == 0), stop=(c == 3))
                            P.op("pe", fn, reads=[UR(35), UR(36), "ones1"], writes=[("ps", cbm), ("ps", cbq)])
                        P.op("dve", lambda e, V_=V_, cbm=cbm, cbq=cbq: e.tensor_scalar(out=mean_sb[:, :], in0=ps[cbm][:, :], scalar1=1.0 / 512, scalar2=None, op0=ALU.mult),
                             reads=[("ps", cbm)], writes=["mean"])
                        P.op("dve", lambda e, V_=V_, cbm=cbm, cbq=cbq: e.tensor_tensor(out=rcp, in0=mean_sb[:, :], in1=mean_sb[:, :], op=ALU.mult), reads=["mean"], writes=[("lt", 0)])
                        P.op("dve", lambda e, V_=V_, cbm=cbm, cbq=cbq: e.scalar_tensor_tensor(out=rstd_sb[:, :], in0=ps[cbq][:, :], scalar=1.0 / 512, in1=rcp, op0=ALU.mult, op1=ALU.subtract),
                             reads=[("ps", cbq), ("lt", 0)], writes=["rstd"])
                        P.op("act", lambda e, V_=V_, cbm=cbm, cbq=cbq: e.activation(out=rstd_sb[:, :], in_=rstd_sb[:, :], func=AF.Sqrt, bias=vcol("eps_ln"), scale=1.0),
                             reads=["rstd", "vecs"], writes=["rstd"])
                        P.op("dve", lambda e, V_=V_, cbm=cbm, cbq=cbq: e.reciprocal(out=rstd_sb[:, :], in_=rstd_sb[:, :]), reads=["rstd"], writes=["rstd"])
                        for c in range(4):
                            i = c % 2
                            P.op("dve", lambda e, V_=V_, c=c, i=i: e.tensor_tensor(out=lt[:, i, :], in0=fa[:, c, 0:T], in1=mean_sb[:, :], op=ALU.subtract),
                                 reads=[("fa", c), "mean"], writes=[("lt", i)])
                            P.op("dve", lambda e, V_=V_, i=i: e.tensor_tensor(out=lt[:, i, :], in0=lt[:, i, :], in1=rstd_sb[:, :], op=ALU.mult),
                                 reads=[("lt", i), "rstd"], writes=[("lt", i)])
                            P.op("act", lambda e, V_=V_, c=c, i=i: e.activation(out=U[:, 8 + c, :], in_=lt[:, i, :], func=AF.Silu, scale=V_("cln_g", c), bias=V_("cln_b", c)),
                                 reads=[("lt", i), "vecs"], writes=[UR(8 + c)])


                set_rot([0, 1, 2, 3, 4])
                for i4 in range(4):
                    wv, wr = wnext("big")
                    for ci in range(4):
                        oc = i4 * 4 + ci
                        bk = rot()
                        mm_group(bk, ps[bk][:, :], [(wv[:, kc, ci * 128:(ci + 1) * 128], U[:, kc, :]) for kc in range(16)], [wr] + XB)
                        P.op("dve", lambda e, V_=V_, oc=oc, bk=bk: e.scalar_tensor_tensor(out=xres[:, oc, :], in0=xres[:, oc, :], scalar=ALPHA, in1=ps[bk][:, :],
                                                                                   op0=ALU.mult, op1=ALU.add),
                             reads=[("xr", oc), ("ps", bk)], writes=[("xr", oc)])
                layer_norm(("ln1_g", l), ("ln1_b", l))

                for half in range(2):
                    for i11 in range(11):
                        wv, wr = wnext("big")
                        kob = None
                        if half == 0 and i11 == 0:
                            kob = [rot() for _ in range(4)]
                            mm_kouter(kob, wv, [0, 128, 256, 384], wr)
                        for ci in range(2):
                            q = i11 * 2 + ci
                            fc = half * 22 + q
                            accs = []
                            for which in range(2):
                                ch = fc + 44 * which
                                col = ci * 256 + which * 128
                                if kob is not None:
                                    bk = kob[ci * 2 + which]
                                else:
                                    bk = rot()
                                    mm_group(bk, ps[bk][:, :], [(wv[:, kc, col:col + 128], U[:, kc, :]) for kc in range(16)], [wr] + XB)
                                ue = q % 2
                                ue = (2 * q + which) % 2
                                ac = 2 + 2 * which + q % 2
                                hidx = l * 88 + ch
                                fw0 = voff[("fw", l)] + ch * 3
                                P.op("act", lambda e, V_=V_, ue=ue, hidx=hidx: e.activation(out=fa[:, ue, 0:2], in_=halo_up[:, hidx, :], func=AF.Copy),
                                     reads=["hup"], writes=[("fa", ue)])
                                P.op("act", lambda e, V_=V_, ue=ue, bk=bk: e.activation(out=fa[:, ue, 2:2 + T], in_=ps[bk][:, :], func=AF.Copy),
                                     reads=[("ps", bk)], writes=[("fa", ue)])
                                P.op("act", lambda e, V_=V_, ac=ac, bk=bk, fw0=fw0, ch=ch: e.activation(
                                    out=fa[:, ac, 0:T], in_=ps[bk][:, :], func=AF.Identity, scale=vecs[:, fw0 + 2:fw0 + 3], bias=V_("fb", ch)),
                                    reads=[("ps", bk), "vecs"], writes=[("fa", ac)])
                                P.op("act", lambda e, V_=V_, ue=ue, hidx=hidx: e.activation(out=halo_up[:, hidx, :], in_=fa[:, ue, T:T + 2], func=AF.Copy),
                                     reads=[("fa", ue)], writes=["hup"])
                                for k in (1, 0):
                                    P.op("dve", lambda e, V_=V_, ue=ue, ac=ac, k=k, fw0=fw0: e.scalar_tensor_tensor(
                                        out=fa[:, ac, 0:T], in0=fa[:, ue, k:k + T], scalar=vecs[:, fw0 + k:fw0 + k + 1], in1=fa[:, ac, 0:T],
                                        op0=ALU.mult, op1=ALU.add), reads=[("fa", ue), ("fa", ac), "vecs"], writes=[("fa", ac)])
                                accs.append(ac)
                            P.op("act", lambda e, V_=V_, ac=accs[1]: e.activation(out=fa[:, ac, 0:T], in_=fa[:, ac, 0:T], func=AF.Silu),
                                 reads=[("fa", accs[1])], writes=[("fa", accs[1])])
                            P.op("dve", lambda e, V_=V_, q=q, a0=accs[0], a1=accs[1]: e.tensor_tensor(out=U[:, 16 + q, :], in0=fa[:, a0, 0:T], in1=fa[:, a1, 0:T], op=ALU.mult),
                                 reads=[("fa", accs[0]), ("fa", accs[1])], writes=[UR(16 + q)])
                    GR = [UR(16 + q) for q in range(22)]
                    for i8 in range(8):
                        wv, wr = wnext("big")
                        for ci in range(2):
                            oc = i8 * 2 + ci
                            bk = rot()
                            mm_group(bk, ps[bk][:, :], [(wv[:, q, ci * 128:(ci + 1) * 128], U[:, 16 + q, :]) for q in range(22)], [wr] + GR)
                            if half == 0:
                                P.op("dve", lambda e, V_=V_, oc=oc, bk=bk: e.scalar_tensor_tensor(out=xres[:, oc, :], in0=xres[:, oc, :], scalar=ALPHA, in1=ps[bk][:, :],
                                                                                           op0=ALU.mult, op1=ALU.add),
                                     reads=[("xr", oc), ("ps", bk)], writes=[("xr", oc)])
                            else:
                                P.op("dve", lambda e, V_=V_, oc=oc, bk=bk: e.tensor_tensor(out=xres[:, oc, :], in0=xres[:, oc, :], in1=ps[bk][:, :], op=ALU.add),
                                     reads=[("xr", oc), ("ps", bk)], writes=[("xr", oc)])
                layer_norm(("ln2_g", l), ("ln2_b", l))

            P.op("sp", lambda e, b=b, t0=t0: e.dma_start(
                out=outT[b * D:(b + 1) * D, t0:t0 + T].rearrange("(c p) t -> p c t", p=128), in_=xres[:, :, :]),
                reads=[("xr", c) for c in range(16)], dma=True)

    assert wstate["next"] == len(blocks)
    P.finish("sp")
    P.emit(nc)
    es.close()
    return nc


def _pack_vecs(L, inp):
    voff, NV = vec_layout(L)
    v = np.zeros((128, NV), np.float32)

    def put(name, arr):
        o = voff[name]
        v[:, o:o + arr.shape[1]] = arr

    col = lambda a: np.ascontiguousarray(np.asarray(a, np.float32).reshape(-1, 128).T)
    put("ln_in_g", col(inp["ln_in_g"]))
    put("ln_in_b", col(inp["ln_in_b"]))
    inv = (1.0 / (10000.0 ** (np.arange(0, 64, 2, dtype=np.float32) / 64.0))).astype(np.float32)
    fr = np.zeros((128, 1), np.float32)
    fr[0:64, 0] = np.concatenate([inv, inv])
    put("invfreq", fr)
    sg = np.zeros((128, 1), np.float32)
    sg[0:32] = -1.0
    sg[32:64] = 1.0
    put("sinsign", sg)
    put("invcnt", np.tile((1.0 / np.arange(1, 17, dtype=np.float32))[None, :], (128, 1)))
    put("eps_ln", np.full((128, 1), LN_EPS, np.float32))
    put("eps_rms", np.full((128, 1), RMS_EPS, np.float32))
    for l in range(L):
        put(("ln1_g", l), col(inp["ln1_g"][l]))
        put(("ln1_b", l), col(inp["ln1_b"][l]))
        put(("ln2_g", l), col(inp["ln2_g"][l]))
        put(("ln2_b", l), col(inp["ln2_b"][l]))
        put(("qg", l), col(inp["q_norm_g"][l]))
        put(("kvg", l), col(inp["kv_norm_g"][l]))
        cw = np.asarray(inp["conv_w"][l], np.float32)
        put(("conv_w", l), np.ascontiguousarray(cw.T.reshape(4, 128, 31).transpose(1, 0, 2).reshape(128, 124)))
        put(("conv_b", l), col(inp["conv_b"][l]))
        put(("cln_g", l), col(inp["conv_ln_g"][l]))
        put(("cln_b", l), col(inp["conv_ln_b"][l]))
        put(("pscale", l), col(inp["pool_scale"][l]))
        fw = np.asarray(inp["ffn_conv_w"][l], np.float32)
        put(("fw", l), np.ascontiguousarray(fw.T.reshape(88, 128, 3).transpose(1, 0, 2).reshape(128, 264)))
        put(("fb", l), col(inp["ffn_conv_b"][l]))
    return v


def _prep_weights(L, inp):
    f = lambda a: np.asarray(a, np.float32)
    idx_in = list(range(0, 832)) + list(range(800, 832)) + list(range(768, 800))
    for c in range(4):
        idx_in += list(range(832 + c * 128, 832 + (c + 1) * 128)) + list(range(1344 + c * 128, 1344 + (c + 1) * 128))
    idx_in += list(range(1856, 2368))
    idx_in = np.array(idx_in)
    assert idx_in.size == IN_AUG
    w_in = np.ascontiguousarray(f(inp["w_in"])[:L][:, :, idx_in]).reshape(L * D, IN_AUG)
    idx_h = np.array(list(range(0, 192)) + list(range(160, 192)) + list(range(128, 160)))
    w_uq = np.ascontiguousarray(f(inp["w_uq"])[:L][:, :, :, idx_h]).reshape(L * 512, 2048)
    w_ukv = np.ascontiguousarray(f(inp["w_ukv"])[:L]).reshape(L * 256, 2048)
    w_out = np.ascontiguousarray(f(inp["w_out"])[:L]).reshape(L * D, D)
    idx_up = []
    for fc in range(44):
        idx_up += list(range(fc * 128, (fc + 1) * 128)) + list(range(DFF + fc * 128, DFF + (fc + 1) * 128))
    idx_up = np.array(idx_up)
    w_up = np.ascontiguousarray(f(inp["w_up"])[:L][:, :, idx_up]).reshape(L * D, 2 * DFF)
    w_down = np.ascontiguousarray(f(inp["w_down"])[:L]).reshape(L * DFF, D)
    w_pool = np.ascontiguousarray(f(inp["w_pool"])[:L]).reshape(L * 512, 128)
    return dict(w_in=w_in, w_uq=w_uq, w_ukv=w_ukv, w_out=w_out, w_up=w_up, w_down=w_down, w_pool=w_pool)


def run(inp, L, S, n_cores, nseq):
    x = np.asarray(inp["x"], np.float32)
    positions = np.asarray(inp["positions"], np.int32)
    shared = _prep_weights(L, inp)
    shared["vecs"] = _pack_vecs(L, inp)
    shared["tri"] = np.triu(np.ones((128, 128), np.float32))
    nc = build(L, S, nseq)
    in_maps = []
    for cidx in range(n_cores):
        xs = x[cidx * nseq:(cidx + 1) * nseq]
        m = dict(shared)
        m["xT"] = np.ascontiguousarray(xs.transpose(0, 2, 1)).reshape(nseq * D, S)
        m["pos"] = np.ascontiguousarray(positions[cidx * nseq:(cidx + 1) * nseq])
        in_maps.append(m)
    res = run_bass_kernel_spmd(nc, in_maps, core_ids=list(range(n_cores)))
    outs = []
    for cidx in range(n_cores):
        o = res.results[cidx]["outT"].reshape(nseq, D, S).transpose(0, 2, 1)
        outs.append(o)
    return np.ascontiguousarray(np.concatenate(outs, axis=0)).astype(np.float32)


def kernel(**inputs):
    return run(inputs, 4, 2048, 8, 2)
```
